# Optimizing a Trainium2 kernel written in Bass

```python
import math
import jax, jax.numpy as jnp
from jax import lax
import numpy as np

D_MODEL = 1024
BATCH = 16
SEQ = 4096
DEPTH = 1
DEC_BATCH = 8
DEC_SEQ = 16
PAST_LEN = 1024

CHUNK = 64
N_HEADS = 8
QK_NOPE = 128
QK_ROPE = 64
QK_HEAD = QK_NOPE + QK_ROPE
V_HEAD = 128
Q_LORA = 512
KV_LORA = 512
ROPE_THETA = 10000.0
SCALE = QK_HEAD ** -0.5
Q_BLOCK = 128
NEG_INF = -1e30
D_RNN = D_MODEL
RNN_BLOCKS = 8
RNN_BLOCK = D_RNN // RNN_BLOCKS
CONV_W = 4
LRU_C = 8.0
D_FF = -(-8 * D_MODEL // (3 * 256)) * 256
EPS = 1e-6
OFF_Q = 0
OFF_KV = OFF_Q + Q_LORA
OFF_KR = OFF_KV + KV_LORA
OFF_X = OFF_KR + QK_ROPE
OFF_GA = OFF_X + D_RNN
OFF_GB = OFF_GA + D_MODEL
IN_TOTAL = OFF_GB + D_MODEL

kernel_name = 'hybrid_mla_rglru_stream_step'


def rms_norm(x, g):
    xf = x.astype(jnp.float32)
    y = xf * lax.rsqrt(jnp.mean(xf * xf, axis=-1, keepdims=True) + EPS)
    return (y * g.astype(jnp.float32)).astype(x.dtype)


def apply_rope(x, pos):
    half = QK_ROPE // 2
    inv_freq = jnp.exp(-math.log(ROPE_THETA) * jnp.arange(half, dtype=jnp.float32) / half)
    ang = pos.astype(jnp.float32)[:, None] * inv_freq[None, :]
    cos = jnp.cos(ang)[None, :, None, :]
    sin = jnp.sin(ang)[None, :, None, :]
    xf = x.astype(jnp.float32)
    x1, x2 = xf[..., :half], xf[..., half:]
    return jnp.concatenate([x1 * cos - x2 * sin, x1 * sin + x2 * cos], axis=-1).astype(x.dtype)


def mla_queries(c_q, pos, q_norm_g, w_uq, qk_q_g):
    b, t, _ = c_q.shape
    q = (rms_norm(c_q, q_norm_g) @ w_uq).reshape(b, t, N_HEADS, QK_HEAD)
    q = rms_norm(q, qk_q_g)
    return jnp.concatenate([q[..., :QK_NOPE], apply_rope(q[..., QK_NOPE:], pos)], axis=-1)


def mla_keys_values(c_kv, k_rope_raw, pos, w_ukv, qk_k_g):
    b, t, _ = c_kv.shape
    kv = (c_kv @ w_ukv).reshape(b, t, N_HEADS, QK_NOPE + V_HEAD)
    k_rope = jnp.broadcast_to(k_rope_raw[:, :, None, :], (b, t, N_HEADS, QK_ROPE))
    k = rms_norm(jnp.concatenate([kv[..., :QK_NOPE], k_rope], axis=-1), qk_k_g)
    k = jnp.concatenate([k[..., :QK_NOPE], apply_rope(k[..., QK_NOPE:], pos)], axis=-1)
    return k, kv[..., QK_NOPE:]


def chunk_causal_attend(q, q_pos, k, v, k_pos):
    s = jnp.einsum('bqhd,bkhd->bhqk', q, k, preferred_element_type=jnp.float32) * SCALE
    allowed = (k_pos[None, :] // CHUNK) <= (q_pos[:, None] // CHUNK)
    s = jnp.where(allowed[None, None], s, NEG_INF)
    p = jax.nn.softmax(s, axis=-1).astype(v.dtype)
    return jnp.einsum('bhqk,bkhd->bqhd', p, v)


def blocked_prompt_attention(q, k, v, pos):
    b, t, h, dk = q.shape
    nb = t // Q_BLOCK
    q_blocks = q.reshape(b, nb, Q_BLOCK, h, dk).transpose(1, 0, 2, 3, 4)
    pos_blocks = pos.reshape(nb, Q_BLOCK)
    out = lax.map(lambda qp: chunk_causal_attend(qp[0], qp[1], k, v, pos), (q_blocks, pos_blocks))
    return out.transpose(1, 0, 2, 3, 4).reshape(b, t, h, V_HEAD)


def causal_depthwise_conv(x, hist, w, bias):
    t = x.shape[1]
    xp = jnp.concatenate([hist, x], axis=1)
    y = bias + xp[:, 0:t] * w[0]
    for j in range(1, CONV_W):
        y = y + xp[:, j:j + t] * w[j]
    return y, xp[:, xp.shape[1] - (CONV_W - 1):]


def rg_lru(xc, h0, w_a, b_a, w_x, b_x, lam):
    b, t, c = xc.shape
    xb = xc.reshape(b, t, RNN_BLOCKS, RNN_BLOCK)
    r = jax.nn.sigmoid((jnp.einsum('btni,nij->btnj', xb, w_a).reshape(b, t, c) + b_a).astype(jnp.float32))
    i = jax.nn.sigmoid((jnp.einsum('btni,nij->btnj', xb, w_x).reshape(b, t, c) + b_x).astype(jnp.float32))
    log_a = -LRU_C * r * jax.nn.softplus(-lam.astype(jnp.float32))
    a = jnp.exp(log_a)
    u = jnp.sqrt(-jnp.expm1(2.0 * log_a)) * (i * xc.astype(jnp.float32))
    u = u.at[:, 0].add(a[:, 0] * h0.astype(jnp.float32))

    def combine(left, right):
        a_l, u_l = left
        a_r, u_r = right
        return a_l * a_r, a_r * u_l + u_r

    _, h = lax.associative_scan(combine, (a, u), axis=1)
    return h.astype(xc.dtype), h[:, -1].astype(xc.dtype)


def hybrid_layer(x, pos, past, p):
    b, t, _ = x.shape
    xn = rms_norm(x, p['norm1_g'])
    z = xn @ p['w_in']
    c_q = z[..., OFF_Q:OFF_KV]
    c_kv = rms_norm(z[..., OFF_KV:OFF_KR], p['kv_norm_g'])
    k_rope_raw = z[..., OFF_KR:OFF_X]
    x_rnn = z[..., OFF_X:OFF_GA]
    g_att = z[..., OFF_GA:OFF_GB]
    g_rnn = z[..., OFF_GB:IN_TOTAL]

    q = mla_queries(c_q, pos, p['q_norm_g'], p['w_uq'], p['qk_q_g'])
    if past is None:
        k, v = mla_keys_values(c_kv, k_rope_raw, pos, p['w_ukv'], p['qk_k_g'])
        att = blocked_prompt_attention(q, k, v, pos)
        conv_hist = jnp.zeros((b, CONV_W - 1, D_RNN), x.dtype)
        h0 = jnp.zeros((b, D_RNN), x.dtype)
    else:
        past_ckv, past_krope, past_pos, conv_hist, h0 = past
        k_pos = jnp.concatenate([past_pos, pos])
        k, v = mla_keys_values(jnp.concatenate([past_ckv, c_kv], axis=1),
                               jnp.concatenate([past_krope, k_rope_raw], axis=1),
                               k_pos, p['w_ukv'], p['qk_k_g'])
        att = chunk_causal_attend(q, pos, k, v, k_pos)

    xc, conv_new = causal_depthwise_conv(x_rnn, conv_hist, p['conv_w'], p['conv_b'])
    h_seq, h_last = rg_lru(xc, h0, p['w_rg_a'], p['b_rg_a'], p['w_rg_x'], p['b_rg_x'], p['lru_lambda'])

    o_att = att.reshape(b, t, N_HEADS * V_HEAD) @ p['w_proj_attn']
    o_rnn = h_seq @ p['w_proj_rnn']
    merged = jax.nn.sigmoid(g_att) * o_att + jax.nn.sigmoid(g_rnn) * o_rnn
    x = x + merged @ p['w_out']

    xn2 = rms_norm(x, p['norm2_g'])
    x = x + (jax.nn.silu(xn2 @ p['w_ffn_gate']) * (xn2 @ p['w_ffn_up'])) @ p['w_ffn_down']
    return x, c_kv, k_rope_raw, conv_new, h_last


def setup_inputs(seed: int = 0) -> dict:
    key = jax.random.key(seed)
    ks = jax.random.split(key, 32)

    def nrm(k, shape, scale):
        return jax.random.normal(k, shape, jnp.float32) * scale

    u = jax.random.uniform(ks[20], (DEPTH, D_RNN), jnp.float32, minval=0.9, maxval=0.999)
    a_base = u ** (1.0 / LRU_C)
    lru_lambda = jnp.log(a_base) - jnp.log1p(-a_base)
    return {
        'x_prompt': nrm(ks[0], (BATCH, SEQ, D_MODEL), 1.0),
        'x_sample': nrm(ks[1], (DEC_BATCH, DEC_SEQ, D_MODEL), 1.0),
        'cache_ckv': nrm(ks[2], (DEPTH, DEC_BATCH, PAST_LEN, KV_LORA), 1.0),
        'cache_krope': nrm(ks[3], (DEPTH, DEC_BATCH, PAST_LEN, QK_ROPE), 1.0),
        'state_conv': nrm(ks[4], (DEPTH, DEC_BATCH, CONV_W - 1, D_RNN), 1.0),
        'state_h': nrm(ks[5], (DEPTH, DEC_BATCH, D_RNN), 0.5),
        'norm1_g': 1.0 + nrm(ks[6], (DEPTH, D_MODEL), 0.02),
        'w_in': nrm(ks[7], (DEPTH, D_MODEL, IN_TOTAL), D_MODEL ** -0.5),
        'q_norm_g': 1.0 + nrm(ks[8], (DEPTH, Q_LORA), 0.02),
        'w_uq': nrm(ks[9], (DEPTH, Q_LORA, N_HEADS * QK_HEAD), Q_LORA ** -0.5),
        'kv_norm_g': 1.0 + nrm(ks[10], (DEPTH, KV_LORA), 0.02),
        'w_ukv': nrm(ks[11], (DEPTH, KV_LORA, N_HEADS * (QK_NOPE + V_HEAD)), KV_LORA ** -0.5),
        'qk_q_g': 1.0 + nrm(ks[12], (DEPTH, QK_HEAD), 0.02),
        'qk_k_g': 1.0 + nrm(ks[13], (DEPTH, QK_HEAD), 0.02),
        'conv_w': nrm(ks[14], (DEPTH, CONV_W, D_RNN), CONV_W ** -0.5),
        'conv_b': nrm(ks[15], (DEPTH, D_RNN), 0.02),
        'w_rg_a': nrm(ks[16], (DEPTH, RNN_BLOCKS, RNN_BLOCK, RNN_BLOCK), RNN_BLOCK ** -0.5),
        'b_rg_a': nrm(ks[17], (DEPTH, D_RNN), 0.02),
        'w_rg_x': nrm(ks[18], (DEPTH, RNN_BLOCKS, RNN_BLOCK, RNN_BLOCK), RNN_BLOCK ** -0.5),
        'b_rg_x': nrm(ks[19], (DEPTH, D_RNN), 0.02),
        'lru_lambda': lru_lambda,
        'w_proj_attn': nrm(ks[21], (DEPTH, N_HEADS * V_HEAD, D_MODEL), (N_HEADS * V_HEAD) ** -0.5),
        'w_proj_rnn': nrm(ks[22], (DEPTH, D_RNN, D_MODEL), D_RNN ** -0.5),
        'w_out': nrm(ks[23], (DEPTH, D_MODEL, D_MODEL), D_MODEL ** -0.5),
        'norm2_g': 1.0 + nrm(ks[24], (DEPTH, D_MODEL), 0.02),
        'w_ffn_gate': nrm(ks[25], (DEPTH, D_MODEL, D_FF), D_MODEL ** -0.5),
        'w_ffn_up': nrm(ks[26], (DEPTH, D_MODEL, D_FF), D_MODEL ** -0.5),
        'w_ffn_down': nrm(ks[27], (DEPTH, D_FF, D_MODEL), D_FF ** -0.5),
    }


def reference(x_prompt, x_sample, cache_ckv, cache_krope, state_conv, state_h,
              norm1_g, w_in, q_norm_g, w_uq, kv_norm_g, w_ukv, qk_q_g, qk_k_g,
              conv_w, conv_b, w_rg_a, b_rg_a, w_rg_x, b_rg_x, lru_lambda,
              w_proj_attn, w_proj_rnn, w_out, norm2_g, w_ffn_gate, w_ffn_up, w_ffn_down):
    past_len = cache_ckv.shape[2]
    pos_prompt = jnp.arange(x_prompt.shape[1], dtype=jnp.int32)
    past_pos = jnp.arange(past_len, dtype=jnp.int32)
    pos_sample = past_len + jnp.arange(x_sample.shape[1], dtype=jnp.int32)

    y_p, y_s = x_prompt, x_sample
    ckv_p, kr_p, conv_p, h_p = [], [], [], []
    ckv_s, kr_s, conv_s, h_s = [], [], [], []
    for l in range(DEPTH):
        p = dict(norm1_g=norm1_g[l], w_in=w_in[l], q_norm_g=q_norm_g[l], w_uq=w_uq[l],
                 kv_norm_g=kv_norm_g[l], w_ukv=w_ukv[l], qk_q_g=qk_q_g[l], qk_k_g=qk_k_g[l],
                 conv_w=conv_w[l], conv_b=conv_b[l], w_rg_a=w_rg_a[l], b_rg_a=b_rg_a[l],
                 w_rg_x=w_rg_x[l], b_rg_x=b_rg_x[l], lru_lambda=lru_lambda[l],
                 w_proj_attn=w_proj_attn[l], w_proj_rnn=w_proj_rnn[l], w_out=w_out[l],
                 norm2_g=norm2_g[l], w_ffn_gate=w_ffn_gate[l], w_ffn_up=w_ffn_up[l],
                 w_ffn_down=w_ffn_down[l])
        y_p, c1, k1, v1, h1 = hybrid_layer(y_p, pos_prompt, None, p)
        past = (cache_ckv[l], cache_krope[l], past_pos, state_conv[l], state_h[l])
        y_s, c2, k2, v2, h2 = hybrid_layer(y_s, pos_sample, past, p)
        ckv_p.append(c1); kr_p.append(k1); conv_p.append(v1); h_p.append(h1)
        ckv_s.append(c2); kr_s.append(k2); conv_s.append(v2); h_s.append(h2)

    new_ckv_prompt = jnp.stack(ckv_p)
    new_krope_prompt = jnp.stack(kr_p)
    new_conv_prompt = jnp.stack(conv_p)
    new_h_prompt = jnp.stack(h_p)
    new_ckv_sample = jnp.stack(ckv_s)
    new_krope_sample = jnp.stack(kr_s)
    new_conv_sample = jnp.stack(conv_s)
    new_h_sample = jnp.stack(h_s)
    return (y_p, y_s, new_ckv_prompt, new_krope_prompt, new_conv_prompt, new_h_prompt,
            new_ckv_sample, new_krope_sample, new_conv_sample, new_h_sample)
```

```python
import math
from contextlib import ExitStack

import numpy as np
import concourse.bass as bass
import concourse.mybir as mybir
from concourse.bass_utils import run_bass_kernel_spmd

F32 = mybir.dt.float32
BF16 = mybir.dt.bfloat16
AF = mybir.ActivationFunctionType
ALU = mybir.AluOpType

D = 1024
H = 8
CHUNK = 64
EPS = 1e-6
SCALE = 192 ** -0.5
D_FF = 2816
NFF = 22
P3_OFF = 22
MAXACT = 2
OFF_Q, OFF_KV, OFF_KR, OFF_X, OFF_GA, OFF_GB = 0, 512, 1024, 1088, 2112, 3136
PAST = 1024
DEC = 16
NCORES = 8

S_W1 = 0
S_W2 = 9
S_W3 = 25
NS = 141
W3_XR, W3_RG, W3_MG, W3_OUT, W3_FF, W3_DN = 0, 8, 10, 42, 50, 94

VC = {}
_o = 0
for _n, _w in [("g1T", 8), ("g2T", 8), ("gqT", 4), ("gq_n", 1), ("gq_r", 1), ("gq_rp", 1), ("gk_n", 1),
               ("gk_r", 1), ("gk_rp", 1), ("convw", 32), ("convb", 8), ("ba", 8), ("bx", 8), ("lam", 8),
               ("conv0", 24), ("h0", 8)]:
    VC[_n] = (_o, _o + _w)
    _o += _w
NV = _o


class Eng:
    def __init__(self, name, h, sem):
        self.name, self.h, self.sem, self.count, self.waited = name, h, sem, 0, {}


class DmaQ:
    def __init__(self, name, eng, sems):
        self.name, self.eng, self.sems, self.n = name, eng, sems, 0
        self.last = {}


class Prog:
    def __init__(self, nc, es):
        self.nc = nc
        mk = lambda n: es.enter_context(nc.semaphore(n))
        self.pe = Eng("pe", nc.tensor, mk("s_pe"))
        self.act = Eng("act", nc.scalar, mk("s_act"))
        self.dve = Eng("dve", nc.vector, mk("s_dve"))
        self.pool = Eng("pool", nc.gpsimd, mk("s_pool"))
        self.sp = Eng("sp", nc.sync, mk("s_sp"))
        self.qw = DmaQ("qw", self.sp, [mk(f"s_qw{i}") for i in range(8)])
        self.qa = DmaQ("qa", self.pool, [mk(f"s_qa{i}") for i in range(8)])
        self.qc = DmaQ("qc", self.pool, [mk(f"s_qc{i}") for i in range(6)])
        self.lastw = {}
        self.readers = {}

    def _need(self, eng, tok):
        key, sem, val, src = tok
        if src == "pe" and eng.name == "pe":
            return
        if eng.waited.get(key, 0) >= val:
            return
        eng.h.wait_ge(sem, val)
        eng.waited[key] = val

    def _deps(self, eng, reads, writes, is_dma):
        for r in reads:
            t = self.lastw.get(r)
            if t is not None:
                self._need(eng, t)
        for w in writes:
            t = self.lastw.get(w)
            if t is not None:
                self._need(eng, t)
            for t in self.readers.get(w, {}).values():
                self._need(eng, t)

    def _commit(self, tok, reads, writes):
        for r in reads:
            self.readers.setdefault(r, {})[tok[0]] = tok
        for w in writes:
            self.lastw[w] = tok
            self.readers[w] = {}

    def op(self, eng, fn, reads=(), writes=()):
        self._deps(eng, reads, writes, False)
        ins = fn()
        eng.count += 1
        ins.then_inc(eng.sem, 1)
        tok = (eng.name, eng.sem, eng.count, eng.name)
        self._commit(tok, reads, writes)
        return tok

    def dma(self, q, out, in_, reads=(), writes=(), **kw):
        K = len(q.sems)
        i = q.n
        slot = i % K
        key = (q.name, slot)
        if i >= K:
            self._need(q.eng, (key, q.sems[slot], 16 * (i // K), None))
        self._deps(q.eng, reads, writes, True)
        ins = q.eng.h.dma_start(out=out, in_=in_, **kw)
        ins.then_inc(q.sems[slot], 16)
        q.n += 1
        tok = (key, q.sems[slot], 16 * (i // K + 1), None)
        q.last[slot] = tok
        self._commit(tok, reads, writes)
        return tok

    def barrier(self, engines=None, queues=None):
        engines = engines or [self.pe, self.act, self.dve, self.pool]
        toks = [(e.name, e.sem, e.count, e.name + "!") for e in [self.pe, self.act, self.dve, self.pool] if e.count]
        for q in (queues or [self.qa]):
            toks += list(q.last.values())
        for e in engines:
            for t in toks:
                if t[0] == e.name:
                    continue
                self._need(e, t)


def build_program(SEQ=4096, NSEQ=2, QCH=2048):
    nc = bass.Bass("TRN2", target_bir_lowering=False)
    TP = SEQ * NSEQ
    NPOS = max(SEQ, PAST)
    TCOL = NPOS + DEC

    def din(name, shape):
        return nc.dram_tensor(name, shape, F32, kind="ExternalInput").ap()

    def dout(name, shape):
        return nc.dram_tensor(name, shape, F32, kind="ExternalOutput").ap()

    xp = din("xp", [TP, D]); xs = din("xs", [DEC, D])
    cckv = din("cckv", [PAST, 512]); ckr2 = din("ckr2", [PAST, 128])
    vecs_d = din("vecs", [128, NV]); gkvbc_d = din("gkvbc", [128, 512]); ident_d = din("ident", [128, 128])
    tabC_d = din("tabC", [64, TCOL]); tabS_d = din("tabS", [64, TCOL])
    wslab = din("wslab", [NS, 128, 1024])
    y_p = dout("y_p", [TP, D]); ckv_p = dout("ckv_p", [TP, 512]); kr_p = dout("kr_p", [TP, 64])
    conv_p = dout("conv_p", [NSEQ, 128, 24]); h_p = dout("h_p", [NSEQ, 128, 8])
    y_s = dout("y_s", [DEC, D]); ckv_s = dout("ckv_s", [DEC, 512]); kr_s = dout("kr_s", [DEC, 64])
    conv_s = dout("conv_s", [1, 128, 24]); h_s = dout("h_s", [1, 128, 8])
    WS = nc.dram_tensor("WS", [NS, 128, 1024], BF16, kind="Internal").ap()
    TKs = max(SEQ, PAST + DEC)
    KTs = nc.dram_tensor("KTs", [H, 128, TKs], BF16, kind="Internal").ap()
    KRs = nc.dram_tensor("KRs", [H, 64, TKs], BF16, kind="Internal").ap()
    Vs = nc.dram_tensor("Vs", [H, 128, (TKs + 127) // 128, 128], BF16, kind="Internal").ap()

    TK = max(SEQ, PAST + DEC)

    with ExitStack() as es:
        P = Prog(nc, es)
        pe, act, dve, pool = P.pe, P.act, P.dve, P.pool

        uid = [0]

        def sbt(stack, name, shape, dt):
            uid[0] += 1
            return stack.enter_context(nc.sbuf_tensor(f"{name}_{uid[0]}", shape, dt))

        vecs = sbt(es, "vecs_sb", [128, NV], F32)
        dvec = sbt(es, "dvec", [128, 24], F32)
        ident_f = sbt(es, "ident_f", [128, 128], F32)
        ident_b = sbt(es, "ident_b", [128, 128], BF16)
        ones_b = sbt(es, "ones_b", [128, 128], BF16)
        ones_f = sbt(es, "ones_f", [128, 128], F32)
        NR = 8
        ring = sbt(es, "ring", [128, NR, 1024], BF16)
        rgw = sbt(es, "rgw", [128, 2, 1024], BF16)
        TKc = max(QCH, PAST + DEC)
        cst = {"kb": 0}
        OT = sbt(es, "OT", [128, H, QCH], BF16)
        NTMP = 10
        tmpf = sbt(es, "tmpf", [128, NTMP, 512], F32)
        stat = sbt(es, "stat", [128, 16, 4], F32)
        hstate = sbt(es, "hstate", [128, 8], F32)
        xcarry = sbt(es, "xcarry", [128, 8, 3], F32)
        banks = [es.enter_context(nc.psum_tensor(f"bank{i}", [128, 512], F32)) for i in range(8)]

        def V(name):
            a, b = VC[name]
            return vecs[:, a:b]

        st = {"bank": 0, "tmp": 0, "stat": 0, "pd": 0}
        held = set()

        def bank():
            while True:
                i = st["bank"]
                st["bank"] = (i + 1) % 8
                if i not in held:
                    return i

        free_tmps = list(range(NTMP))

        def tmp():
            assert free_tmps, "out of tmp tiles"
            return free_tmps.pop(0)

        def rel(*ts):
            for t in ts:
                assert t not in free_tmps
                free_tmps.append(t)

        def need_tmps(k):
            while len(free_tmps) < k:
                yield

        def stat_slot():
            i = st["stat"]
            st["stat"] = (i + 1) % 16
            return i

        def mm(out, lhsT, rhs, start, stop, reads, writes):
            return P.op(pe, lambda: nc.tensor.matmul(out, lhsT=lhsT, rhs=rhs, start=start, stop=stop), reads, writes)

        def tr(out, in_, idn, reads, writes):
            return P.op(pe, lambda: nc.tensor.transpose(out, in_, idn), reads, writes)

        def A(eng, out, in_, func, reads, writes, **kw):
            return P.op(eng, lambda: nc.scalar.activation(out=out, in_=in_, func=func, **kw), reads, writes)

        def rstd_from_ssq(ssq_ap, ln_ap, rstd_ap, n, res):
            A(act, ln_ap, ssq_ap, AF.Ln, [res], [res], scale=1.0 / n, bias=EPS)
            A(act, rstd_ap, ln_ap, AF.Exp, [res], [res], scale=-0.5)

        chunks_of = lambda T: [(c0, min(QCH, T - c0)) for c0 in range(0, T, QCH)]
        blocks_of = lambda n: [(b0, min(512, n - b0)) for b0 in range(0, n, 512)]
        jobs = []
        for s in range(NSEQ):
            jobs.append(dict(kind="p", x=xp[s * SEQ:(s + 1) * SEQ, :], T=SEQ, past=0, y=y_p[s * SEQ:(s + 1) * SEQ, :],
                             ckv=ckv_p[s * SEQ:(s + 1) * SEQ, :], kr=kr_p[s * SEQ:(s + 1) * SEQ, :],
                             conv=conv_p[s], h=h_p[s], tcol0=0))
        jobs.append(dict(kind="s", x=xs, T=DEC, past=PAST, y=y_s, ckv=ckv_s, kr=kr_s, conv=conv_s[0], h=h_s[0],
                         tcol0=NPOS))
        sp_ = {"n": 0}

        def fetch(idx):
            i = sp_["n"]
            sp_["n"] += 1
            slot = i % NR
            P.dma(P.qw, ring[:, slot, :], WS[idx], reads=[f"WS{idx // 4}"], writes=[f"ring{slot}"])
            return ring[:, slot, :], f"ring{slot}"

        def interleave(gens, offset, max_active=MAXACT):
            active, pending, steps = [], list(gens), {}
            while active or pending:
                if pending and len(active) < max_active and (not active or steps[id(active[-1])] >= offset):
                    g = pending.pop(0)
                    active.append(g)
                    steps[id(g)] = 0
                for g in list(active):
                    try:
                        next(g)
                        steps[id(g)] += 1
                    except StopIteration:
                        active.remove(g)
                yield

        P.dma(P.qa, vecs[:], vecs_d, writes=["vecs"])
        P.dma(P.qa, ident_f[:], ident_d, writes=["ident_f"])
        P.op(dve, lambda: nc.vector.tensor_copy(out=ident_b[:], in_=ident_f[:]), ["ident_f"], ["ident_b"])
        P.op(pool, lambda: nc.gpsimd.memset(ones_b[:], 1.0), [], ["ones_b"])
        P.op(pool, lambda: nc.gpsimd.memset(ones_f[:], 1.0), [], ["ones_f"])
        a0, a1 = VC["ba"]
        P.op(dve, lambda: nc.vector.tensor_scalar(out=dvec[:, 0:16], in0=vecs[:, a0:a0 + 16], scalar1=-1.0, scalar2=None,
                                                 op0=ALU.mult), ["vecs"], ["dvec_a"])
        A(act, dvec[:, 16:24], V("lam"), AF.Exp, ["vecs"], ["dvec_c"], scale=-1.0)
        A(act, dvec[:, 16:24], dvec[:, 16:24], AF.Ln, ["dvec_c"], ["dvec_c"], scale=1.0, bias=1.0)
        P.op(dve, lambda: nc.vector.tensor_scalar(out=dvec[:, 16:24], in0=dvec[:, 16:24], scalar1=-8.0, scalar2=None,
                                                 op0=ALU.mult), ["dvec_c"], ["dvec_c"])
        CG = 4
        for g0 in range(0, NS, CG):
            g1 = min(NS, g0 + CG)
            P.dma(P.qc, WS[g0:g1], wslab[g0:g1], writes=[f"WS{g0 // CG}"])
        P.dma(P.qa, rgw[:], WS[S_W3 + W3_RG:S_W3 + W3_RG + 2].rearrange("s p f -> p s f"),
              reads=[f"WS{(S_W3 + W3_RG) // CG}", f"WS{(S_W3 + W3_RG + 1) // CG}"], writes=["rgw"])

        def tcol(jb, kidx):
            if jb["kind"] == "p":
                return kidx
            return kidx if kidx < PAST else NPOS + (kidx - PAST)

        def kr_feature_path(jb, kr_f, kr_res, nt, k0, tabs):
            tc_, ts_, tres = tabs
            b = bank()
            bk = banks[b]
            tr(bk[0:64, 0:nt], kr_f[0:nt, 0:64], ident_f[0:nt, 0:nt], [kr_res], [f"bank{b}"])
            tr(bk[0:64, 128:128 + nt], kr_f[0:nt, 64:128], ident_f[0:nt, 0:nt], [kr_res], [f"bank{b}"])
            t1, t2 = tmp(), tmp()
            P.op(dve, lambda: nc.vector.scalar_tensor_tensor(out=tmpf[0:64, t1, 0:nt], in0=bk[0:64, 0:nt],
                                                             scalar=V("gk_r")[0:64, :], in1=tc_, op0=ALU.mult, op1=ALU.mult),
                 [f"bank{b}", tres], [f"tmp{t1}"])
            P.op(dve, lambda: nc.vector.scalar_tensor_tensor(out=tmpf[0:64, t2, 0:nt], in0=bk[0:64, 128:128 + nt],
                                                             scalar=V("gk_rp")[0:64, :], in1=ts_, op0=ALU.mult, op1=ALU.mult),
                 [f"bank{b}", tres], [f"tmp{t2}"])
            P.op(pool, lambda: nc.gpsimd.tensor_tensor(out=krT[:, k0 - cst["kb"]:k0 - cst["kb"] + nt], in0=tmpf[0:64, t1, 0:nt],
                                                       in1=tmpf[0:64, t2, 0:nt], op=ALU.add),
                 [f"tmp{t1}", f"tmp{t2}"], [f"krT{k0 // 128}"])
            rel(t1, t2)
            A(act, sqkr[:, k0 - cst["kb"]:k0 - cst["kb"] + nt], bk[0:64, 0:nt], AF.Square, [f"bank{b}"], [f"sqkr{k0 // 128}"])

        def phase0(jb):
            with ExitStack() as ps_:
                cin = sbt(ps_, "cin", [128, 2, 512], F32)
                cb = sbt(ps_, "cb", [128, 2, 512], BF16)
                kin = sbt(ps_, "kin", [128, 2, 128], F32)
                tab = sbt(ps_, "tab0", [64, 2, 2, 128], F32)
                for ti in range(PAST // 128):
                    b2 = ti % 2
                    k0 = ti * 128
                    P.dma(P.qa, cin[:, b2, :], cckv[k0:k0 + 128, :], writes=[f"cin{b2}"])
                    P.dma(P.qa, kin[:, b2, :], ckr2[k0:k0 + 128, :], writes=[f"kin{b2}"])
                    P.dma(P.qa, tab[:, b2, 0, :], tabC_d[:, k0:k0 + 128], writes=[f"tab{b2}"])
                    P.dma(P.qa, tab[:, b2, 1, :], tabS_d[:, k0:k0 + 128], writes=[f"tab{b2}"])
                    P.op(dve, lambda: nc.vector.tensor_copy(out=cb[:, b2, :], in_=cin[:, b2, :]), [f"cin{b2}"], [f"cb{b2}"])
                    bi = bank()
                    bb = banks[bi][:].bitcast(BF16)
                    for kc in range(4):
                        tr(bb[:, kc * 128:(kc + 1) * 128], cb[:, b2, kc * 128:(kc + 1) * 128], ident_b[:],
                           [f"cb{b2}", "ident_b"], [f"bank{bi}"])
                    A(act, ckvT[:, :, k0 - cst["kb"]:k0 - cst["kb"] + 128], bb[:, 0:512].rearrange("p (a b) -> p a b", a=4), AF.Copy,
                      [f"bank{bi}"], [f"ckvT{ti}"])
                    kr_feature_path(jb, kin[:, b2, :], f"kin{b2}", 128, k0, (tab[:, b2, 0, :], tab[:, b2, 1, :], f"tab{b2}"))
            P.barrier()

        def phase1(jb, c0, cn):
            past = jb["past"]
            with ExitStack() as ps_:
                W1 = sbt(ps_, "W1", [128, 8, 1152], BF16)
                xin = sbt(ps_, "xin", [128, 2, 1024], F32)
                junk = sbt(ps_, "junk", [128, 2, 1024], BF16)
                xsb = sbt(ps_, "xsb", [128, 2, 1024], BF16)
                xnTt = sbt(ps_, "xnTt", [128, 2, 8, 128], BF16)
                cqs = sbt(ps_, "cqs", [128, 2, 512], BF16)
                ckvf = sbt(ps_, "ckvf", [128, 2, 512], F32)
                ckvb = sbt(ps_, "ckvb", [128, 2, 512], BF16)
                krf = sbt(ps_, "krf", [128, 2, 128], F32)
                tab = sbt(ps_, "tab1", [64, 2, 2, 128], F32)
                gkvbc = sbt(ps_, "gkvbc_sb", [128, 512], F32)
                P.dma(P.qa, gkvbc[:], gkvbc_d, writes=["gkvbc"])
                P.dma(P.qa, W1[:].rearrange("p a b -> p (a b)").rearrange("p (s f) -> p s f", s=9),
                      WS[S_W1:S_W1 + 9].rearrange("s p f -> p s f"), reads=["WS0", "WS1", "WS2"], writes=["W1"])
                nt = min(128, cn)
                def tile_gen(ti):
                    b2 = ti % 2
                    t0 = c0 + ti * nt
                    k0 = past + t0
                    tc0 = tcol(jb, k0)
                    P.dma(P.qa, xin[0:nt, b2, :], jb["x"][t0:t0 + nt, :], writes=[f"xin{b2}"])
                    P.dma(P.qa, tab[:, b2, 0, 0:nt], tabC_d[:, tc0:tc0 + nt], writes=[f"tab{b2}"])
                    P.dma(P.qa, tab[:, b2, 1, 0:nt], tabS_d[:, tc0:tc0 + nt], writes=[f"tab{b2}"])
                    ss = stat_slot()
                    sr = f"stat{ss}"
                    A(act, junk[0:nt, b2, :], xin[0:nt, b2, :], AF.Square, [f"xin{b2}"], [f"junk{b2}", sr], accum_out=stat[0:nt, ss, 0:1])
                    rstd_from_ssq(stat[0:nt, ss, 0:1], stat[0:nt, ss, 1:2], stat[0:nt, ss, 2:3], D, sr)
                    P.op(dve, lambda: nc.vector.tensor_scalar(out=xsb[0:nt, b2, :], in0=xin[0:nt, b2, :], scalar1=stat[0:nt, ss, 2:3],
                                                             scalar2=None, op0=ALU.mult), [f"xin{b2}", sr], [f"xsb{b2}"])
                    bi = bank()
                    bb = banks[bi][:].bitcast(BF16)
                    for kc in range(8):
                        tr(bb[:, kc * nt:(kc + 1) * nt], xsb[0:nt, b2, kc * 128:(kc + 1) * 128], ident_b[0:nt, 0:nt],
                           [f"xsb{b2}"], [f"bank{bi}"])
                    P.op(dve, lambda: nc.vector.tensor_tensor(out=xnTt[:, b2, :, 0:nt],
                                                             in0=bb[:, 0:8 * nt].rearrange("p (a b) -> p a b", a=8),
                                                             in1=V("g1T").unsqueeze(2).to_broadcast([128, 8, nt]), op=ALU.mult),
                         [f"bank{bi}"], [f"xnTt{b2}"])
                    yield
                    bq, bk_, br = bank(), bank(), bank()
                    for kc in range(8):
                        mm(banks[bq][0:nt, :], xnTt[:, b2, kc, 0:nt], W1[:, kc, 0:512], kc == 0, kc == 7,
                           [f"xnTt{b2}", "W1"], [f"bank{bq}"])
                    for kc in range(8):
                        mm(banks[bk_][0:nt, :], xnTt[:, b2, kc, 0:nt], W1[:, kc, 512:1024], kc == 0, kc == 7,
                           [f"xnTt{b2}", "W1"], [f"bank{bk_}"])
                    for kc in range(8):
                        mm(banks[br][0:nt, 0:128], xnTt[:, b2, kc, 0:nt], W1[:, kc, 1024:1152], kc == 0, kc == 7,
                           [f"xnTt{b2}", "W1"], [f"bank{br}"])
                    yield
                    ss = stat_slot(); sr = f"stat{ss}"
                    A(act, junk[0:nt, b2, 0:512], banks[bq][0:nt, :], AF.Square, [f"bank{bq}"], [f"junk{b2}", sr],
                      accum_out=stat[0:nt, ss, 0:1])
                    rstd_from_ssq(stat[0:nt, ss, 0:1], stat[0:nt, ss, 1:2], stat[0:nt, ss, 2:3], 512, sr)
                    P.op(dve, lambda: nc.vector.tensor_scalar(out=cqs[0:nt, b2, :], in0=banks[bq][0:nt, :], scalar1=stat[0:nt, ss, 2:3],
                                                             scalar2=None, op0=ALU.mult), [f"bank{bq}", sr], [f"cqs{b2}"])
                    bi = bank()
                    bb = banks[bi][:].bitcast(BF16)
                    for kc in range(4):
                        tr(bb[:, kc * nt:(kc + 1) * nt], cqs[0:nt, b2, kc * 128:(kc + 1) * 128], ident_b[0:nt, 0:nt],
                           [f"cqs{b2}"], [f"bank{bi}"])
                    tl = t0 - c0
                    P.op(dve, lambda: nc.vector.tensor_tensor(out=cqT[:, :, tl:tl + nt],
                                                             in0=bb[:, 0:4 * nt].rearrange("p (a b) -> p a b", a=4),
                                                             in1=V("gqT").unsqueeze(2).to_broadcast([128, 4, nt]), op=ALU.mult),
                         [f"bank{bi}"], [f"cqT{tl // 128}"])
                    yield
                    ss = stat_slot(); sr = f"stat{ss}"
                    A(act, junk[0:nt, b2, 512:1024], banks[bk_][0:nt, :], AF.Square, [f"bank{bk_}"], [f"junk{b2}", sr],
                      accum_out=stat[0:nt, ss, 0:1])
                    rstd_from_ssq(stat[0:nt, ss, 0:1], stat[0:nt, ss, 1:2], stat[0:nt, ss, 2:3], 512, sr)
                    P.op(dve, lambda: nc.vector.scalar_tensor_tensor(out=ckvf[0:nt, b2, :], in0=banks[bk_][0:nt, :],
                                                                    scalar=stat[0:nt, ss, 2:3], in1=gkvbc[0:nt, :],
                                                                    op0=ALU.mult, op1=ALU.mult),
                         [f"bank{bk_}", sr, "gkvbc"], [f"ckvf{b2}"])
                    P.dma(P.qa, jb["ckv"][t0:t0 + nt, :], ckvf[0:nt, b2, :], reads=[f"ckvf{b2}"], writes=["out_ckv"])
                    P.op(dve, lambda: nc.vector.tensor_copy(out=ckvb[0:nt, b2, :], in_=ckvf[0:nt, b2, :]), [f"ckvf{b2}"], [f"ckvb{b2}"])
                    bi = bank()
                    bb = banks[bi][:].bitcast(BF16)
                    for kc in range(4):
                        tr(bb[:, kc * nt:(kc + 1) * nt], ckvb[0:nt, b2, kc * 128:(kc + 1) * 128], ident_b[0:nt, 0:nt],
                           [f"ckvb{b2}"], [f"bank{bi}"])
                    A(act, ckvT[:, :, k0 - cst["kb"]:k0 - cst["kb"] + nt], bb[:, 0:4 * nt].rearrange("p (a b) -> p a b", a=4), AF.Copy,
                      [f"bank{bi}"], [f"ckvT{k0 // 128}"])
                    yield
                    A(act, krf[0:nt, b2, :], banks[br][0:nt, 0:128], AF.Copy, [f"bank{br}"], [f"krf{b2}"])
                    P.dma(P.qa, jb["kr"][t0:t0 + nt, :], krf[0:nt, b2, 0:64], reads=[f"krf{b2}"], writes=["out_kr"])
                    kr_feature_path(jb, krf[:, b2, :], f"krf{b2}", nt, k0, (tab[:, b2, 0, 0:nt], tab[:, b2, 1, 0:nt], f"tab{b2}"))

                for _ in interleave([tile_gen(ti) for ti in range(cn // nt)], 2):
                    pass
            P.barrier()

        def phase2(jb, c0, cn):
            past = jb["past"]
            kend = past + c0 + cn
            with ExitStack() as ps_:
                NKT = (TK + 127) // 128
                KT = sbt(ps_, "KT", [128, 2, TK], BF16)
                KR = sbt(ps_, "KR", [128, 2, TK], BF16)
                Vh = sbt(ps_, "Vh", [128, 2, NKT, 128], BF16)
                qn = sbt(ps_, "qn", [128, 4, 512], BF16)
                qr = sbt(ps_, "qr", [128, 4, 512], BF16)
                sqa = sbt(ps_, "sqa", [128, 4, 512], BF16)
                sqb = sbt(ps_, "sqb", [64, 2, 512], BF16)
                pbuf = sbt(ps_, "pbuf", [128, 4, 512], BF16)
                pdiag = sbt(ps_, "pdiag", [128, 2, 512], BF16)
                tabq = sbt(ps_, "tabq", [64, 2, QCH], F32)
                accb = sbt(ps_, "accb", [128, 2, 2, 512], F32)
                cnt = {"sq": 0, "q": 0, "p": 0, "pd": 0}
                P.op(pool, lambda: nc.gpsimd.memset(pdiag[:], 0.0), [], ["pdiag0", "pdiag1"])
                P.op(pool, lambda: nc.gpsimd.memset(KR[64:128, :, 0:kend], 0.0), [], ["KRpad"])
                P.op(pool, lambda: nc.gpsimd.memset(qr[64:128, :, :], 0.0), [], ["qrpad"])
                tq0 = tcol(jb, past + c0)
                P.dma(P.qa, tabq[:, 0, 0:cn], tabC_d[:, tq0:tq0 + cn], writes=["tabq"])
                P.dma(P.qa, tabq[:, 1, 0:cn], tabS_d[:, tq0:tq0 + cn], writes=["tabq"])

                def norm_rstd(bA, n, sq_rope_ap, sq_rope_res):
                    s4 = cnt["sq"] % 4
                    cnt["sq"] += 1
                    A(act, sqa[:, s4, 0:n], banks[bA][:, 0:n], AF.Square, [f"bank{bA}"], [f"sqa{s4}"])
                    bD = bank()
                    mm(banks[bD][:, 0:n], ones_b[:, :], sqa[:, s4, 0:n], True, False, [f"sqa{s4}", "ones_b"], [f"bank{bD}"])
                    mm(banks[bD][:, 0:n], ones_b[0:64, :], sq_rope_ap, False, True, sq_rope_res + ["ones_b"], [f"bank{bD}"])
                    tl_, tr_ = tmp(), tmp()
                    A(act, tmpf[:, tl_, 0:n], banks[bD][:, 0:n], AF.Ln, [f"bank{bD}"], [f"tmp{tl_}"], scale=1.0 / 192, bias=EPS)
                    A(act, tmpf[:, tr_, 0:n], tmpf[:, tl_, 0:n], AF.Exp, [f"tmp{tl_}"], [f"tmp{tr_}"], scale=-0.5)
                    rel(tl_)
                    return tr_

                def head_gen(h):
                    hs = h % 2
                    wq, wq_res = fetch(S_W2 + 2 * h)
                    wkv, wkv_res = fetch(S_W2 + 2 * h + 1)
                    wq3 = wq.rearrange("p (a b) -> p a b", a=4)
                    wkv3 = wkv.rearrange("p (a b) -> p a b", a=4)
                    kold = (past + c0) if (jb["kind"] == "p" and c0 > 0) else 0
                    if kold:
                        to = kold // 128
                        P.dma(P.qa, KT[:, hs, 0:kold], KTs[h, :, 0:kold], reads=[f"KTs{h}"], writes=[f"KT{hs}_{i}" for i in range(to)])
                        P.dma(P.qa, KR[0:64, hs, 0:kold], KRs[h, :, 0:kold], reads=[f"KRs{h}"], writes=[f"KR{hs}_{i}" for i in range(to)])
                        P.dma(P.qa, Vh[:, hs, 0:to, :], Vs[h, :, 0:to, :], reads=[f"Vs{h}"], writes=[f"Vh{hs}_{i}" for i in range(to)])
                    for (kb0, kn) in [(kold + b0_, n_) for (b0_, n_) in blocks_of(kend - kold)]:
                        trng = range(kb0 // 128, (kb0 + kn + 127) // 128)
                        ckv_res = [f"ckvT{i}" for i in trng]
                        bA = bank()
                        for kc in range(4):
                            mm(banks[bA][:, 0:kn], wkv3[:, kc, 0:128], ckvT[:, kc, kb0 - cst["kb"]:kb0 - cst["kb"] + kn], kc == 0, kc == 3,
                               [wkv_res] + ckv_res, [f"bank{bA}"])
                        bV = bank()
                        ntl = (kn + 127) // 128
                        ktn = min(128, kn)
                        for j in range(ntl):
                            for kc in range(4):
                                mm(banks[bV][0:ktn, j * 128:(j + 1) * 128], ckvT[:, kc, kb0 - cst["kb"] + j * 128:kb0 - cst["kb"] + j * 128 + ktn],
                                   wkv3[:, kc, 128:256], kc == 0, kc == 3, [wkv_res] + ckv_res, [f"bank{bV}"])
                        A(act, Vh[0:ktn, hs, kb0 // 128:kb0 // 128 + ntl, :],
                          banks[bV][0:ktn, 0:ntl * 128].rearrange("p (a b) -> p a b", a=ntl), AF.Copy,
                          [f"bank{bV}"], [f"Vh{hs}_{i}" for i in range(kb0 // 128, kb0 // 128 + ntl)])
                        rs = norm_rstd(bA, kn, sqkr[:, kb0 - cst["kb"]:kb0 - cst["kb"] + kn], [f"sqkr{i}" for i in trng])
                        P.op(dve, lambda: nc.vector.scalar_tensor_tensor(out=KT[:, hs, kb0:kb0 + kn], in0=banks[bA][:, 0:kn],
                                                                        scalar=V("gk_n"), in1=tmpf[:, rs, 0:kn],
                                                                        op0=ALU.mult, op1=ALU.mult),
                             [f"bank{bA}", f"tmp{rs}"], [f"KT{hs}_{i}" for i in trng])
                        P.op(pool, lambda: nc.gpsimd.tensor_tensor(out=KR[0:64, hs, kb0:kb0 + kn], in0=krT[:, kb0 - cst["kb"]:kb0 - cst["kb"] + kn],
                                                                 in1=tmpf[0:64, rs, 0:kn], op=ALU.mult),
                             [f"tmp{rs}"] + [f"krT{i}" for i in trng], [f"KR{hs}_{i}" for i in trng])
                        rel(rs)
                        if jb["kind"] == "p" and c0 + cn < jb["T"]:
                            t0_, t1_ = kb0 // 128, (kb0 + kn) // 128
                            P.dma(P.qa, KTs[h, :, kb0:kb0 + kn], KT[:, hs, kb0:kb0 + kn], reads=[f"KT{hs}_{i}" for i in trng], writes=[f"KTs{h}"])
                            P.dma(P.qa, KRs[h, :, kb0:kb0 + kn], KR[0:64, hs, kb0:kb0 + kn], reads=[f"KR{hs}_{i}" for i in trng], writes=[f"KRs{h}"])
                            P.dma(P.qa, Vs[h, :, t0_:t1_, :], Vh[:, hs, t0_:t1_, :], reads=[f"Vh{hs}_{i}" for i in trng], writes=[f"Vs{h}"])
                        yield
                    for (qb0, qnn) in blocks_of(cn):
                        qs = hs * 2 + (qb0 // 512) % 2
                        cq_res = [f"cqT{i}" for i in range(qb0 // 128, (qb0 + qnn + 127) // 128)]
                        bA, bB, bC = bank(), bank(), bank()
                        for kc in range(4):
                            mm(banks[bA][:, 0:qnn], wq3[:, kc, 0:128], cqT[:, kc, qb0:qb0 + qnn], kc == 0, kc == 3,
                               [wq_res] + cq_res, [f"bank{bA}"])
                        for kc in range(4):
                            mm(banks[bB][0:64, 0:qnn], wq3[:, kc, 128:192], cqT[:, kc, qb0:qb0 + qnn], kc == 0, kc == 3,
                               [wq_res] + cq_res, [f"bank{bB}"])
                        for kc in range(4):
                            mm(banks[bC][0:64, 0:qnn], wq3[:, kc, 192:256], cqT[:, kc, qb0:qb0 + qnn], kc == 0, kc == 3,
                               [wq_res] + cq_res, [f"bank{bC}"])
                        A(act, sqb[:, hs, 0:qnn], banks[bB][0:64, 0:qnn], AF.Square, [f"bank{bB}"], [f"sqb{hs}"])
                        rs = norm_rstd(bA, qnn, sqb[:, hs, 0:qnn], [f"sqb{hs}"])
                        P.op(dve, lambda: nc.vector.scalar_tensor_tensor(out=qn[:, qs, 0:qnn], in0=banks[bA][:, 0:qnn],
                                                                        scalar=V("gq_n"), in1=tmpf[:, rs, 0:qnn],
                                                                        op0=ALU.mult, op1=ALU.mult),
                             [f"bank{bA}", f"tmp{rs}"], [f"qn{qs}"])
                        t1, t2 = tmp(), tmp()
                        P.op(dve, lambda: nc.vector.scalar_tensor_tensor(out=tmpf[0:64, t1, 0:qnn], in0=banks[bB][0:64, 0:qnn],
                                                                        scalar=V("gq_r")[0:64, :], in1=tabq[:, 0, qb0:qb0 + qnn],
                                                                        op0=ALU.mult, op1=ALU.mult),
                             [f"bank{bB}", "tabq"], [f"tmp{t1}"])
                        P.op(dve, lambda: nc.vector.scalar_tensor_tensor(out=tmpf[0:64, t2, 0:qnn], in0=banks[bC][0:64, 0:qnn],
                                                                        scalar=V("gq_rp")[0:64, :], in1=tabq[:, 1, qb0:qb0 + qnn],
                                                                        op0=ALU.mult, op1=ALU.mult),
                             [f"bank{bC}", "tabq"], [f"tmp{t2}"])
                        P.op(pool, lambda: nc.gpsimd.tensor_tensor(out=tmpf[0:64, t1, 0:qnn], in0=tmpf[0:64, t1, 0:qnn],
                                                                   in1=tmpf[0:64, t2, 0:qnn], op=ALU.add),
                             [f"tmp{t1}", f"tmp{t2}"], [f"tmp{t1}"])
                        P.op(pool, lambda: nc.gpsimd.tensor_tensor(out=qr[0:64, qs, 0:qnn], in0=tmpf[0:64, t1, 0:qnn],
                                                                   in1=tmpf[0:64, rs, 0:qnn], op=ALU.mult),
                             [f"tmp{t1}", f"tmp{rs}"], [f"qr{qs}"])
                        rel(rs, t1, t2)
                        yield
                        tiles = []
                        if jb["kind"] == "p":
                            qg = (c0 + qb0) // 512
                            for kt in range(4 * qg):
                                tiles.append((kt, 128, None))
                            for jj in range((qnn + 127) // 128):
                                tiles.append((4 * qg + jj, 128, jj))
                        else:
                            for kt in range((kend + 127) // 128):
                                tiles.append((kt, min(128, kend - kt * 128), None))
                        bO = bank()
                        held.add(bO)
                        sb_ = {}
                        accD, accP = accb[:, hs, 0, :], accb[:, hs, 1, :]
                        rD, rP = f"accD{hs}", f"accP{hs}"
                        P.op(dve, lambda: nc.vector.memset(accD, 0.0), [], [rD])
                        P.op(pool, lambda: nc.gpsimd.memset(accP, 0.0), [], [rP])

                        def emit_S(i):
                            kt, ktn, jj = tiles[i]
                            cc0 = 0 if jj is None else 128 * jj
                            W = qnn - cc0
                            b = bank()
                            sb_[i] = b
                            mm(banks[b][0:ktn, 0:W], KT[:, hs, kt * 128:kt * 128 + ktn], qn[:, qs, cc0:qnn], True, False,
                               [f"KT{hs}_{kt}", f"qn{qs}"], [f"bank{b}"])
                            mm(banks[b][0:ktn, 0:W], KR[:, hs, kt * 128:kt * 128 + ktn], qr[:, qs, cc0:qnn], False, True,
                               [f"KR{hs}_{kt}", f"qr{qs}", "KRpad", "qrpad"], [f"bank{b}"])

                        def emit_PV(i):
                            kt, ktn, jj = tiles[i]
                            cc0 = 0 if jj is None else 128 * jj
                            W = qnn - cc0
                            b = sb_.pop(i)
                            if jj is None:
                                s4 = cnt["p"] % 4
                                cnt["p"] += 1
                                pap, pres = pbuf[:, s4, :], f"pbuf{s4}"
                                A(act, pap[0:ktn, 0:W], banks[b][0:ktn, 0:W], AF.Exp, [f"bank{b}"], [pres], scale=SCALE)
                            else:
                                s2 = cnt["pd"] % 2
                                cnt["pd"] += 1
                                pap, pres = pdiag[:, s2, :], f"pdiag{s2}"
                                A(act, pap[0:64, 0:W], banks[b][0:64, 0:W], AF.Exp, [f"bank{b}"], [pres], scale=SCALE)
                                if W > 64:
                                    A(act, pap[64:128, 64:W], banks[b][64:128, 64:W], AF.Exp, [f"bank{b}"], [pres], scale=SCALE)
                            first, last = (i == 0), (i == len(tiles) - 1)
                            mm(banks[bO][:, cc0:qnn], Vh[0:ktn, hs, kt, :], pap[0:ktn, 0:W], first, last,
                               [f"Vh{hs}_{kt}", pres], [f"bank{bO}"])
                            if i % 3 != 2:
                                P.op(dve, lambda: nc.vector.tensor_tensor(out=accD[0:ktn, cc0:qnn], in0=accD[0:ktn, cc0:qnn],
                                                                         in1=pap[0:ktn, 0:W], op=ALU.add), [pres, rD], [rD])
                            else:
                                P.op(pool, lambda: nc.gpsimd.tensor_tensor(out=accP[0:ktn, cc0:qnn], in0=accP[0:ktn, cc0:qnn],
                                                                           in1=pap[0:ktn, 0:W], op=ALU.add), [pres, rP], [rP])

                        nT = len(tiles)
                        for i in range(min(2, nT)):
                            emit_S(i)
                        for i in range(nT):
                            emit_PV(i)
                            if i + 2 < nT:
                                emit_S(i + 2)
                            if i % 2 == 1:
                                yield
                        bD = bank()
                        mm(banks[bD][:, 0:qnn], ones_f[:, :], accD[:, 0:qnn], True, False, [rD], [f"bank{bD}"])
                        mm(banks[bD][:, 0:qnn], ones_f[:, :], accP[:, 0:qnn], False, True, [rP], [f"bank{bD}"])
                        rd = tmp()
                        A(act, tmpf[:, rd, 0:qnn], banks[bD][:, 0:qnn], AF.Ln, [f"bank{bD}"], [f"tmp{rd}"])
                        A(act, tmpf[:, rd, 0:qnn], tmpf[:, rd, 0:qnn], AF.Exp, [f"tmp{rd}"], [f"tmp{rd}"], scale=-1.0)
                        P.op(dve, lambda: nc.vector.tensor_tensor(out=OT[:, h, qb0:qb0 + qnn], in0=banks[bO][:, 0:qnn],
                                                                 in1=tmpf[:, rd, 0:qnn], op=ALU.mult),
                             [f"bank{bO}", f"tmp{rd}"], [f"OT{h}_{qb0 // 512}"])
                        rel(rd)
                        held.discard(bO)
                        yield

                nkb = len(blocks_of(kend - ((past + c0) if (jb["kind"] == "p" and c0 > 0) else 0)))
                for _ in interleave([head_gen(h) for h in range(H)], nkb + 1, 2):
                    pass
            P.barrier()

        def phase3(jb, c0, cn, first_chunk, last_chunk):
            with ExitStack() as ps_:
                xy = sbt(ps_, "xy", [128, 2, 4, 1024], F32)
                XN = sbt(ps_, "XN", [128, 2, 8, 512], BF16)
                RZ = sbt(ps_, "RZ", [128, 44, 512], BF16)
                xsb = sbt(ps_, "xsb3", [128, 2, 1024], BF16)
                xr = sbt(ps_, "xr", [128, 2, 515], F32)
                xcb = sbt(ps_, "xcb", [128, 2, 512], BF16)
                blks = blocks_of(cn)
                cw0 = VC["convw"][0]
                cnt3 = {"xs": 0}

                def rz(s, k):
                    return k + 22 * s

                def load_x(bi_, b0, n):
                    nt = min(128, n)
                    for t in range(n // nt):
                        P.dma(P.qa, xy[0:nt, bi_ % 2, t, :], jb["x"][c0 + b0 + t * nt:c0 + b0 + (t + 1) * nt, :],
                              writes=[f"xy{bi_ % 2}_{t}"])

                def norm_T(src_ap, src_res, nt, t, gname, s):
                    ss = stat_slot(); sr = f"stat{ss}"
                    xs_ = cnt3["xs"] % 2
                    cnt3["xs"] += 1
                    A(act, xsb[0:nt, xs_, :], src_ap, AF.Square, [src_res], [f"xsb{xs_}", sr], accum_out=stat[0:nt, ss, 0:1])
                    rstd_from_ssq(stat[0:nt, ss, 0:1], stat[0:nt, ss, 1:2], stat[0:nt, ss, 2:3], D, sr)
                    P.op(dve, lambda: nc.vector.tensor_scalar(out=xsb[0:nt, xs_, :], in0=src_ap, scalar1=stat[0:nt, ss, 2:3],
                                                             scalar2=None, op0=ALU.mult), [src_res, sr], [f"xsb{xs_}"])
                    bi = bank()
                    bb = banks[bi][:].bitcast(BF16)
                    for kc in range(8):
                        tr(bb[:, kc * nt:(kc + 1) * nt], xsb[0:nt, xs_, kc * 128:(kc + 1) * 128], ident_b[0:nt, 0:nt],
                           [f"xsb{xs_}"], [f"bank{bi}"])
                    P.op(dve, lambda: nc.vector.tensor_tensor(out=XN[:, s, :, t * nt:(t + 1) * nt],
                                                             in0=bb[:, 0:8 * nt].rearrange("p (a b) -> p a b", a=8),
                                                             in1=V(gname).unsqueeze(2).to_broadcast([128, 8, nt]), op=ALU.mult),
                         [f"bank{bi}"], [f"XN{s}_{kc}" for kc in range(8)])

                def sigmoid_chain(pairs, n):
                    for tt in pairs:
                        A(act, tmpf[:, tt, 0:n], tmpf[:, tt, 0:n], AF.Ln, [f"tmp{tt}"], [f"tmp{tt}"], scale=1.0, bias=1.0)
                    for tt in pairs:
                        A(act, tmpf[:, tt, 0:n], tmpf[:, tt, 0:n], AF.Exp, [f"tmp{tt}"], [f"tmp{tt}"], scale=-1.0)

                def block_gen(bi_, b0, n):
                    s = bi_ % 2
                    nt = min(128, n)
                    ntile = n // nt
                    first_blk = first_chunk and bi_ == 0
                    last_blk = last_chunk and bi_ == len(blks) - 1
                    XNr = [f"XN{s}_{kc}" for kc in range(8)]
                    if bi_ >= 2:
                        load_x(bi_, b0, n)
                    RZr = lambda k: f"RZ{rz(s, k)}"
                    RZa = lambda k: RZ[:, rz(s, k), :]
                    for t in range(ntile):
                        norm_T(xy[0:nt, s, t, :], f"xy{s}_{t}", nt, t, "g1T", s)
                        yield
                    rga3 = rgw[:, 0, :].rearrange("p (a b) -> p a b", a=8)
                    rgx3 = rgw[:, 1, :].rearrange("p (a b) -> p a b", a=8)
                    if first_blk:
                        if jb["kind"] == "p":
                            P.op(pool, lambda: nc.gpsimd.memset(xcarry[:], 0.0), [], ["xcarry"])
                            P.op(pool, lambda: nc.gpsimd.memset(hstate[:], 0.0), [], ["hstate"])
                        else:
                            P.op(pool, lambda: nc.gpsimd.tensor_copy(out=xcarry[:].rearrange("p a b -> p (a b)"), in_=V("conv0")),
                                 ["vecs"], ["xcarry"])
                            P.op(pool, lambda: nc.gpsimd.tensor_copy(out=hstate[:], in_=V("h0")), ["vecs"], ["hstate"])

                    def rnn_gen(chs):
                        for nch in chs:
                            yield from need_tmps(4)
                            w3, wres = fetch(S_W3 + W3_XR + nch)
                            w3 = w3.rearrange("p (a b) -> p a b", a=8)
                            xs_ = nch % 2
                            bx = bank()
                            for kc in range(8):
                                mm(banks[bx][:, 0:n], w3[:, kc, :], XN[:, s, kc, 0:n], kc == 0, kc == 7, [wres] + XNr, [f"bank{bx}"])
                            P.op(pool, lambda: nc.gpsimd.tensor_copy(out=xr[:, xs_, 0:3], in_=xcarry[:, nch, :]), [f"xcarry{nch}"], [f"xr{xs_}"])
                            A(act, xr[:, xs_, 3:3 + n], banks[bx][:, 0:n], AF.Copy, [f"bank{bx}"], [f"xr{xs_}"])
                            P.op(pool, lambda: nc.gpsimd.tensor_copy(out=xcarry[:, nch, :], in_=xr[:, xs_, n:n + 3]), [f"xr{xs_}"], [f"xcarry{nch}"])
                            xc = tmp()
                            cb = cw0 + nch * 4
                            P.op(dve, lambda: nc.vector.tensor_scalar(out=tmpf[:, xc, 0:n], in0=xr[:, xs_, 0:n],
                                                                     scalar1=vecs[:, cb:cb + 1], scalar2=V("convb")[:, nch:nch + 1],
                                                                     op0=ALU.mult, op1=ALU.add), [f"xr{xs_}", "vecs"], [f"tmp{xc}"])
                            for j in range(1, 4):
                                P.op(dve, lambda: nc.vector.scalar_tensor_tensor(out=tmpf[:, xc, 0:n], in0=xr[:, xs_, j:j + n],
                                                                                scalar=vecs[:, cb + j:cb + j + 1], in1=tmpf[:, xc, 0:n],
                                                                                op0=ALU.mult, op1=ALU.add),
                                     [f"xr{xs_}", f"tmp{xc}"], [f"tmp{xc}"])
                            P.op(dve, lambda: nc.vector.tensor_copy(out=xcb[:, xs_, 0:n], in_=tmpf[:, xc, 0:n]), [f"tmp{xc}"], [f"xcb{xs_}"])
                            yield
                            yield from need_tmps(3)
                            br_, bi2 = bank(), bank()
                            mm(banks[br_][:, 0:n], rga3[:, nch, :], xcb[:, xs_, 0:n], True, True, ["rgw", f"xcb{xs_}"], [f"bank{br_}"])
                            mm(banks[bi2][:, 0:n], rgx3[:, nch, :], xcb[:, xs_, 0:n], True, True, ["rgw", f"xcb{xs_}"], [f"bank{bi2}"])
                            tr_, ti_ = tmp(), tmp()
                            A(act, tmpf[:, tr_, 0:n], banks[br_][:, 0:n], AF.Exp, [f"bank{br_}", "dvec_a"], [f"tmp{tr_}"],
                              scale=-1.0, bias=dvec[:, nch:nch + 1])
                            A(act, tmpf[:, ti_, 0:n], banks[bi2][:, 0:n], AF.Exp, [f"bank{bi2}", "dvec_a"], [f"tmp{ti_}"],
                              scale=-1.0, bias=dvec[:, 8 + nch:9 + nch])
                            sigmoid_chain((tr_, ti_), n)
                            yield
                            ta, tm = tmp(), tr_
                            A(act, tmpf[:, ta, 0:n], tmpf[:, tr_, 0:n], AF.Exp, [f"tmp{tr_}", "dvec_c"], [f"tmp{ta}"],
                              scale=dvec[:, 16 + nch:17 + nch])
                            P.op(pool, lambda: nc.gpsimd.tensor_tensor(out=tmpf[:, tm, 0:n], in0=tmpf[:, ta, 0:n], in1=tmpf[:, ta, 0:n],
                                                                       op=ALU.mult), [f"tmp{ta}"], [f"tmp{tm}"])
                            A(act, tmpf[:, tm, 0:n], tmpf[:, tm, 0:n], AF.Ln, [f"tmp{tm}"], [f"tmp{tm}"], scale=-1.0, bias=1.0)
                            A(act, tmpf[:, tm, 0:n], tmpf[:, tm, 0:n], AF.Exp, [f"tmp{tm}"], [f"tmp{tm}"], scale=0.5)
                            P.op(pool, lambda: nc.gpsimd.tensor_tensor(out=tmpf[:, ti_, 0:n], in0=tmpf[:, ti_, 0:n], in1=tmpf[:, xc, 0:n],
                                                                       op=ALU.mult), [f"tmp{ti_}", f"tmp{xc}"], [f"tmp{ti_}"])
                            P.op(pool, lambda: nc.gpsimd.tensor_tensor(out=tmpf[:, ti_, 0:n], in0=tmpf[:, ti_, 0:n], in1=tmpf[:, tm, 0:n],
                                                                       op=ALU.mult), [f"tmp{ti_}", f"tmp{tm}"], [f"tmp{ti_}"])
                            P.op(dve, lambda: nc.vector.tensor_tensor_scan(out=tmpf[:, xc, 0:n], data0=tmpf[:, ta, 0:n],
                                                                          data1=tmpf[:, ti_, 0:n], initial=hstate[:, nch:nch + 1],
                                                                          op0=ALU.mult, op1=ALU.add),
                                 [f"tmp{ta}", f"tmp{ti_}", f"hstate{nch}", f"tmp{xc}"], [f"tmp{xc}"])
                            P.op(pool, lambda: nc.gpsimd.tensor_copy(out=hstate[:, nch:nch + 1], in_=tmpf[:, xc, n - 1:n]),
                                 [f"tmp{xc}"], [f"hstate{nch}"])
                            A(act, RZa(nch)[:, 0:n], tmpf[:, xc, 0:n], AF.Copy, [f"tmp{xc}"], [RZr(nch)])
                            rel(xc, tr_, ti_, ta)
                            yield

                    if first_blk:
                        for nch in range(8):
                            P.lastw[f"xcarry{nch}"] = P.lastw["xcarry"]
                            P.lastw[f"hstate{nch}"] = P.lastw["hstate"]
                    yield from interleave([rnn_gen(range(0, 8, 2)), rnn_gen(range(1, 8, 2))], 1)
                    if last_blk:
                        P.dma(P.qa, jb["conv"], xcarry[:].rearrange("p a b -> p (a b)"), reads=[f"xcarry{k}" for k in range(8)],
                              writes=["out_conv"])
                        P.dma(P.qa, jb["h"], hstate[:], reads=[f"hstate{k}" for k in range(8)], writes=["out_h"])
                    qbi = b0 // 512
                    for j in range(8):
                        base = S_W3 + W3_MG + 4 * j
                        tparts = []
                        yield from need_tmps(2)
                        for part, (gi, oi) in enumerate(((0, 2), (1, 3))):
                            gsl, gres = fetch(base + gi)
                            osl, ores = fetch(base + oi)
                            g3 = gsl.rearrange("p (a b) -> p a b", a=8)
                            o3 = osl.rearrange("p (a b) -> p a b", a=8)
                            bg, bo = bank(), bank()
                            for kc in range(8):
                                mm(banks[bg][:, 0:n], g3[:, kc, :], XN[:, s, kc, 0:n], kc == 0, kc == 7, [gres] + XNr, [f"bank{bg}"])
                            for kc in range(8):
                                if part == 0:
                                    rhs, rres = OT[:, kc, b0:b0 + n], [f"OT{kc}_{qbi}"]
                                else:
                                    rhs, rres = RZa(kc)[:, 0:n], [RZr(kc)]
                                mm(banks[bo][:, 0:n], o3[:, kc, :], rhs, kc == 0, kc == 7, [ores] + rres, [f"bank{bo}"])
                            te = tmp()
                            A(act, tmpf[:, te, 0:n], banks[bg][:, 0:n], AF.Exp, [f"bank{bg}"], [f"tmp{te}"], scale=-1.0)
                            sigmoid_chain((te,), n)
                            P.op(dve, lambda: nc.vector.tensor_tensor(out=tmpf[:, te, 0:n], in0=banks[bo][:, 0:n], in1=tmpf[:, te, 0:n],
                                                                     op=ALU.mult), [f"bank{bo}", f"tmp{te}"], [f"tmp{te}"])
                            tparts.append(te)
                            yield
                        ea, eg = tparts
                        P.op(pool, lambda: nc.gpsimd.tensor_tensor(out=RZa(8 + j)[:, 0:n], in0=tmpf[:, ea, 0:n], in1=tmpf[:, eg, 0:n],
                                                                   op=ALU.add), [f"tmp{ea}", f"tmp{eg}"], [RZr(8 + j)])
                        rel(ea, eg)

                    def acc_stage(nslab, slab_base, kc_of, src_k, last_kc):
                        for half in range(2):
                            while len(held) + ntile > 6:
                                yield
                            bs = [bank() for _ in range(ntile)]
                            for b_ in bs:
                                held.add(b_)
                            for si in range(nslab):
                                wsl, wres = fetch(slab_base + half * nslab + si)
                                w3 = wsl.rearrange("p (a b) -> p a b", a=2)
                                for kk in range(2):
                                    kc = 2 * si + kk
                                    for t in range(ntile):
                                        mm(banks[bs[t]][0:nt, :], RZa(src_k(kc))[:, t * nt:(t + 1) * nt], w3[:, kk, :], kc == 0, kc == last_kc,
                                           [wres, RZr(src_k(kc))], [f"bank{bs[t]}"])
                                yield
                            for t in range(ntile):
                                P.op(dve, lambda: nc.vector.tensor_tensor(out=xy[0:nt, s, t, half * 512:(half + 1) * 512],
                                                                         in0=banks[bs[t]][0:nt, :],
                                                                         in1=xy[0:nt, s, t, half * 512:(half + 1) * 512], op=ALU.add),
                                     [f"bank{bs[t]}", f"xy{s}_{t}"], [f"xy{s}_{t}"])
                                held.discard(bs[t])
                            yield

                    yield from acc_stage(4, S_W3 + W3_OUT, None, lambda kc: 8 + kc, 7)
                    for t in range(ntile):
                        norm_T(xy[0:nt, s, t, :], f"xy{s}_{t}", nt, t, "g2T", s)
                        yield
                    for j in range(NFF):
                        yield from need_tmps(1)
                        gsl, gres = fetch(S_W3 + W3_FF + 2 * j)
                        usl, ures = fetch(S_W3 + W3_FF + 2 * j + 1)
                        g3 = gsl.rearrange("p (a b) -> p a b", a=8)
                        u3 = usl.rearrange("p (a b) -> p a b", a=8)
                        bg, bu = bank(), bank()
                        for kc in range(8):
                            mm(banks[bg][:, 0:n], g3[:, kc, :], XN[:, s, kc, 0:n], kc == 0, kc == 7, [gres] + XNr, [f"bank{bg}"])
                        for kc in range(8):
                            mm(banks[bu][:, 0:n], u3[:, kc, :], XN[:, s, kc, 0:n], kc == 0, kc == 7, [ures] + XNr, [f"bank{bu}"])
                        te = tmp()
                        A(act, tmpf[:, te, 0:n], banks[bg][:, 0:n], AF.Exp, [f"bank{bg}"], [f"tmp{te}"], scale=-1.0)
                        sigmoid_chain((te,), n)
                        P.op(dve, lambda: nc.vector.tensor_tensor(out=tmpf[:, te, 0:n], in0=banks[bg][:, 0:n], in1=tmpf[:, te, 0:n],
                                                                 op=ALU.mult), [f"bank{bg}", f"tmp{te}"], [f"tmp{te}"])
                        P.op(dve, lambda: nc.vector.tensor_tensor(out=RZa(j)[:, 0:n], in0=banks[bu][:, 0:n], in1=tmpf[:, te, 0:n],
                                                                 op=ALU.mult), [f"bank{bu}", f"tmp{te}"], [RZr(j)])
                        rel(te)
                        yield
                    yield from acc_stage(11, S_W3 + W3_DN, None, lambda kc: kc, NFF - 1)
                    for t in range(ntile):
                        P.dma(P.qa, jb["y"][c0 + b0 + t * nt:c0 + b0 + (t + 1) * nt, :], xy[0:nt, s, t, :],
                              reads=[f"xy{s}_{t}"], writes=["out_y"])

                for bi_, (b0, n) in enumerate(blks[:2]):
                    load_x(bi_, b0, n)
                for _ in interleave([block_gen(bi_, b0, n) for bi_, (b0, n) in enumerate(blks)], P3_OFF):
                    pass
            P.barrier()

        for jb in jobs:
            chs = chunks_of(jb["T"])
            for ci, (c0, cn) in enumerate(chs):
                with ExitStack() as cs_:
                    cqT = sbt(cs_, "cqT", [128, 4, QCH], BF16)
                    ckvT = sbt(cs_, "ckvT", [128, 4, TKc], BF16)
                    krT = sbt(cs_, "krT", [64, TKc], BF16)
                    sqkr = sbt(cs_, "sqkr", [64, TKc], BF16)
                    cst["kb"] = c0 if jb["kind"] == "p" else 0
                    if jb["kind"] == "s":
                        phase0(jb)
                    phase1(jb, c0, cn)
                    phase2(jb, c0, cn)
                phase3(jb, c0, cn, ci == 0, ci == len(chs) - 1)
        P.barrier(engines=[P.sp, pool], queues=[P.qa, P.qw])
    return nc


def _slabs(inp):
    w_in = inp["w_in"][0]; w_uq = inp["w_uq"][0]; w_ukv = inp["w_ukv"][0]
    S = np.empty((NS, 128, 1024), np.float32)
    perm = np.concatenate([np.arange(32, 64), np.arange(0, 32)])

    def kc_cols(w, cols):
        K = w.shape[0]
        return w[:, cols].reshape(K // 128, 128, len(cols)).transpose(1, 0, 2)

    kr_cols = np.arange(OFF_KR, OFF_KR + 64)
    cols1 = np.concatenate([np.arange(0, 1024), kr_cols, kr_cols[perm]])
    S[0:9] = kc_cols(w_in, cols1).reshape(128, 9, 1024).transpose(1, 0, 2)
    for h in range(H):
        qc = np.arange(h * 192, h * 192 + 192)
        cols = np.concatenate([qc[:128], qc[128:], qc[128:][perm]])
        S[S_W2 + 2 * h] = kc_cols(w_uq, cols).reshape(128, 1024)
        kvc = np.arange(h * 256, h * 256 + 256)
        S[S_W2 + 2 * h + 1] = kc_cols(w_ukv, kvc).reshape(128, 1024)
    b = S_W3
    for n in range(8):
        S[b + W3_XR + n] = kc_cols(w_in, np.arange(OFF_X + n * 128, OFF_X + (n + 1) * 128)).reshape(128, 1024)
    S[b + W3_RG] = inp["w_rg_a"][0].transpose(1, 0, 2).reshape(128, 1024)
    S[b + W3_RG + 1] = inp["w_rg_x"][0].transpose(1, 0, 2).reshape(128, 1024)
    wpa = inp["w_proj_attn"][0]; wpr = inp["w_proj_rnn"][0]
    for j in range(8):
        cj = np.arange(j * 128, (j + 1) * 128)
        S[b + W3_MG + 4 * j + 0] = kc_cols(w_in, OFF_GA + cj).reshape(128, 1024)
        S[b + W3_MG + 4 * j + 1] = kc_cols(w_in, OFF_GB + cj).reshape(128, 1024)
        S[b + W3_MG + 4 * j + 2] = kc_cols(wpa, cj).reshape(128, 1024)
        S[b + W3_MG + 4 * j + 3] = kc_cols(wpr, cj).reshape(128, 1024)
    wo = inp["w_out"][0]
    for half in range(2):
        for s4 in range(4):
            S[b + W3_OUT + half * 4 + s4] = wo[s4 * 256:(s4 + 1) * 256, half * 512:(half + 1) * 512].reshape(2, 128, 512).transpose(1, 0, 2).reshape(128, 1024)
    wg = inp["w_ffn_gate"][0]; wu = inp["w_ffn_up"][0]; wd = inp["w_ffn_down"][0]
    for j in range(NFF):
        cj = np.arange(j * 128, (j + 1) * 128)
        S[b + W3_FF + 2 * j] = kc_cols(wg, cj).reshape(128, 1024)
        S[b + W3_FF + 2 * j + 1] = kc_cols(wu, cj).reshape(128, 1024)
    for half in range(2):
        for s11 in range(11):
            S[b + W3_DN + half * 11 + s11] = wd[s11 * 256:(s11 + 1) * 256, half * 512:(half + 1) * 512].reshape(2, 128, 512).transpose(1, 0, 2).reshape(128, 1024)
    return S


def _vecs_common(inp):
    v = np.zeros((128, NV), np.float32)
    perm = np.concatenate([np.arange(32, 64), np.arange(0, 32)])

    def put(name, arr):
        a, b = VC[name]
        arr = np.asarray(arr, np.float32)
        v[:arr.shape[0], a:b] = arr.reshape(arr.shape[0], b - a)

    put("g1T", inp["norm1_g"][0].reshape(8, 128).T)
    put("g2T", inp["norm2_g"][0].reshape(8, 128).T)
    put("gqT", inp["q_norm_g"][0].reshape(4, 128).T)
    gq = inp["qk_q_g"][0]; gk = inp["qk_k_g"][0]
    put("gq_n", gq[:128, None]); put("gq_r", gq[128:, None]); put("gq_rp", gq[128:][perm][:, None])
    put("gk_n", gk[:128, None]); put("gk_r", gk[128:, None]); put("gk_rp", gk[128:][perm][:, None])
    put("convw", inp["conv_w"][0].reshape(4, 8, 128).transpose(2, 1, 0).reshape(128, 32))
    put("convb", inp["conv_b"][0].reshape(8, 128).T)
    put("ba", inp["b_rg_a"][0].reshape(8, 128).T)
    put("bx", inp["b_rg_x"][0].reshape(8, 128).T)
    put("lam", inp["lru_lambda"][0].reshape(8, 128).T)
    return v


def _rope_tables(seq):
    half = 32
    inv_freq = np.exp(-math.log(10000.0) * np.arange(half, dtype=np.float32) / half).astype(np.float32)
    pos = np.concatenate([np.arange(max(seq, PAST), dtype=np.float32), PAST + np.arange(DEC, dtype=np.float32)])
    ang = (pos[:, None] * inv_freq[None, :]).astype(np.float32)
    c = np.cos(ang).astype(np.float32).T
    s = np.sin(ang).astype(np.float32).T
    return np.ascontiguousarray(np.concatenate([c, c], 0)), np.ascontiguousarray(np.concatenate([-s, s], 0))


_CACHE = {}


def _in_maps(inp, SEQ, NSEQ, ncores):
    S = _slabs(inp)
    vc = _vecs_common(inp)
    tabC, tabS = _rope_tables(SEQ)
    ident = np.eye(128, dtype=np.float32)
    gkvbc = np.ascontiguousarray(np.broadcast_to(inp["kv_norm_g"][0][None, :], (128, 512))).astype(np.float32)
    perm = np.concatenate([np.arange(32, 64), np.arange(0, 32)])
    in_maps = []
    for c in range(ncores):
        v = vc.copy()
        a, b = VC["conv0"]
        v[:, a:b] = inp["state_conv"][0, c].reshape(3, 8, 128).transpose(2, 1, 0).reshape(128, 24)
        a, b = VC["h0"]
        v[:, a:b] = inp["state_h"][0, c].reshape(8, 128).T
        kr = inp["cache_krope"][0, c]
        in_maps.append(dict(
            xp=np.ascontiguousarray(inp["x_prompt"][c * NSEQ:(c + 1) * NSEQ].reshape(NSEQ * SEQ, D)),
            xs=np.ascontiguousarray(inp["x_sample"][c]),
            cckv=np.ascontiguousarray(inp["cache_ckv"][0, c]),
            ckr2=np.ascontiguousarray(np.concatenate([kr, kr[:, perm]], axis=1)),
            vecs=v, gkvbc=gkvbc, ident=ident, tabC=tabC, tabS=tabS, wslab=S))
    return in_maps


def _run(inp, SEQ, NSEQ, QCH, ncores):
    key = (SEQ, NSEQ, QCH)
    if key not in _CACHE:
        _CACHE[key] = build_program(SEQ, NSEQ, QCH)
    nc = _CACHE[key]
    in_maps = _in_maps(inp, SEQ, NSEQ, ncores)
    res = run_bass_kernel_spmd(nc, in_maps, core_ids=list(range(ncores)))
    return _assemble(res.results, SEQ, NSEQ)


def _assemble(R, SEQ, NSEQ):
    y_p = np.concatenate([r["y_p"].reshape(NSEQ, SEQ, D) for r in R], 0)
    y_s = np.stack([r["y_s"] for r in R], 0)
    ckv_p = np.concatenate([r["ckv_p"].reshape(NSEQ, SEQ, 512) for r in R], 0)[None]
    kr_p = np.concatenate([r["kr_p"].reshape(NSEQ, SEQ, 64) for r in R], 0)[None]

    def conv_fix(a):
        n = a.shape[0]
        return a.reshape(n, 128, 8, 3).transpose(0, 3, 2, 1).reshape(n, 3, 1024)

    def h_fix(a):
        n = a.shape[0]
        return a.transpose(0, 2, 1).reshape(n, 1024)

    conv_pp = np.concatenate([conv_fix(r["conv_p"]) for r in R], 0)[None]
    h_pp = np.concatenate([h_fix(r["h_p"]) for r in R], 0)[None]
    ckv_s = np.stack([r["ckv_s"] for r in R], 0)[None]
    kr_s = np.stack([r["kr_s"] for r in R], 0)[None]
    conv_s = np.concatenate([conv_fix(r["conv_s"]) for r in R], 0)[None]
    h_s = np.concatenate([h_fix(r["h_s"]) for r in R], 0)[None]
    outs = (y_p, y_s, ckv_p, kr_p, conv_pp, h_pp, ckv_s, kr_s, conv_s, h_s)
    return tuple(np.ascontiguousarray(o, dtype=np.float32) for o in outs)


def kernel(**inputs):
    inp = {k: np.asarray(v) for k, v in inputs.items()}
    return _run(inp, 4096, 2, 2048, NCORES)
```

```python
import math
from contextlib import ExitStack

import numpy as np
import concourse.bass as bass
import concourse.mybir as mybir
from concourse.bass_utils import run_bass_kernel_spmd

F32 = mybir.dt.float32
BF16 = mybir.dt.bfloat16
AF = mybir.ActivationFunctionType
ALU = mybir.AluOpType

D = 1024
H = 8
CHUNK = 64
EPS = 1e-6
SCALE = 192 ** -0.5
D_FF = 2816
NFF = 22
P3_OFF = 22
MAXACT = 2
OFF_Q, OFF_KV, OFF_KR, OFF_X, OFF_GA, OFF_GB = 0, 512, 1024, 1088, 2112, 3136
PAST = 1024
DEC = 16
NCORES = 8

S_W1 = 0
S_W2 = 9
S_W3 = 25
NS = 141
W3_XR, W3_RG, W3_MG, W3_OUT, W3_FF, W3_DN = 0, 8, 10, 42, 50, 94

VC = {}
_o = 0
for _n, _w in [("g1T", 8), ("g2T", 8), ("gqT", 4), ("gq_n", 1), ("gq_r", 1), ("gq_rp", 1), ("gk_n", 1),
               ("gk_r", 1), ("gk_rp", 1), ("convw", 32), ("convb", 8), ("ba", 8), ("bx", 8), ("lam", 8),
               ("conv0", 24), ("h0", 8)]:
    VC[_n] = (_o, _o + _w)
    _o += _w
NV = _o


class Eng:
    def __init__(self, name, h, sem):
        self.name, self.h, self.sem, self.count, self.waited = name, h, sem, 0, {}


class DmaQ:
    def __init__(self, name, eng, sems):
        self.name, self.eng, self.sems, self.n = name, eng, sems, 0
        self.last = {}


class Prog:
    def __init__(self, nc, es):
        self.nc = nc
        mk = lambda n: es.enter_context(nc.semaphore(n))
        self.pe = Eng("pe", nc.tensor, mk("s_pe"))
        self.act = Eng("act", nc.scalar, mk("s_act"))
        self.dve = Eng("dve", nc.vector, mk("s_dve"))
        self.pool = Eng("pool", nc.gpsimd, mk("s_pool"))
        self.sp = Eng("sp", nc.sync, mk("s_sp"))
        self.qw = DmaQ("qw", self.sp, [mk(f"s_qw{i}") for i in range(8)])
        self.qa = DmaQ("qa", self.pool, [mk(f"s_qa{i}") for i in range(8)])
        self.qc = DmaQ("qc", self.pool, [mk(f"s_qc{i}") for i in range(6)])
        self.lastw = {}
        self.readers = {}
        self.nwaits = 0

    def _need(self, eng, tok):
        key, sem, val, src = tok[:4]
        if src == "pe" and eng.name == "pe":
            return
        if eng.waited.get(key, 0) >= val:
            return
        eng.h.wait_ge(sem, val)
        self.nwaits += 1
        eng.waited[key] = val
        if len(tok) > 4:
            w = eng.waited
            for k, v in tok[4].items():
                if w.get(k, 0) < v:
                    w[k] = v

    def _deps(self, eng, reads, writes, is_dma):
        for r in reads:
            t = self.lastw.get(r)
            if t is not None:
                self._need(eng, t)
        for w in writes:
            t = self.lastw.get(w)
            if t is not None:
                self._need(eng, t)
            for t in self.readers.get(w, {}).values():
                self._need(eng, t)

    def _commit(self, tok, reads, writes):
        for r in reads:
            self.readers.setdefault(r, {})[tok[0]] = tok
        for w in writes:
            self.lastw[w] = tok
            self.readers[w] = {}

    def op(self, eng, fn, reads=(), writes=()):
        self._deps(eng, reads, writes, False)
        ins = fn()
        eng.count += 1
        ins.then_inc(eng.sem, 1)
        tok = (eng.name, eng.sem, eng.count, eng.name, dict(eng.waited))
        self._commit(tok, reads, writes)
        return tok

    def dma(self, q, out, in_, reads=(), writes=(), **kw):
        K = len(q.sems)
        i = q.n
        slot = i % K
        key = (q.name, slot)
        if i >= K:
            self._need(q.eng, (key, q.sems[slot], 16 * (i // K), None))
        self._deps(q.eng, reads, writes, True)
        ins = q.eng.h.dma_start(out=out, in_=in_, **kw)
        ins.then_inc(q.sems[slot], 16)
        q.n += 1
        tok = (key, q.sems[slot], 16 * (i // K + 1), None, dict(q.eng.waited))
        q.last[slot] = tok
        self._commit(tok, reads, writes)
        return tok

    def barrier(self, engines=None, queues=None):
        engines = engines or [self.pe, self.act, self.dve, self.pool]
        toks = [(e.name, e.sem, e.count, e.name + "!") for e in [self.pe, self.act, self.dve, self.pool] if e.count]
        for q in (queues or [self.qa]):
            toks += list(q.last.values())
        for e in engines:
            for t in toks:
                if t[0] == e.name:
                    continue
                self._need(e, t)


def build_program(SEQ=4096, NSEQ=2, QCH=2048):
    nc = bass.Bass("TRN2", target_bir_lowering=False)
    TP = SEQ * NSEQ
    NPOS = max(SEQ, PAST)
    TCOL = NPOS + DEC

    def din(name, shape):
        return nc.dram_tensor(name, shape, F32, kind="ExternalInput").ap()

    def dout(name, shape):
        return nc.dram_tensor(name, shape, F32, kind="ExternalOutput").ap()

    xp = din("xp", [TP, D]); xs = din("xs", [DEC, D])
    cckv = din("cckv", [PAST, 512]); ckr2 = din("ckr2", [PAST, 128])
    vecs_d = din("vecs", [128, NV]); gkvbc_d = din("gkvbc", [128, 512]); ident_d = din("ident", [128, 128])
    tabC_d = din("tabC", [64, TCOL]); tabS_d = din("tabS", [64, TCOL])
    wslab = din("wslab", [NS, 128, 1024])
    y_p = dout("y_p", [TP, D]); ckv_p = dout("ckv_p", [TP, 512]); kr_p = dout("kr_p", [TP, 64])
    conv_p = dout("conv_p", [NSEQ, 128, 24]); h_p = dout("h_p", [NSEQ, 128, 8])
    y_s = dout("y_s", [DEC, D]); ckv_s = dout("ckv_s", [DEC, 512]); kr_s = dout("kr_s", [DEC, 64])
    conv_s = dout("conv_s", [1, 128, 24]); h_s = dout("h_s", [1, 128, 8])
    WS = nc.dram_tensor("WS", [NS, 128, 1024], BF16, kind="Internal").ap()
    TKs = max(SEQ, PAST + DEC)
    KTs = nc.dram_tensor("KTs", [H, 128, TKs], BF16, kind="Internal").ap()
    KRs = nc.dram_tensor("KRs", [H, 64, TKs], BF16, kind="Internal").ap()
    Vs = nc.dram_tensor("Vs", [H, 128, (TKs + 127) // 128, 128], BF16, kind="Internal").ap()

    TK = max(SEQ, PAST + DEC)

    with ExitStack() as es:
        P = Prog(nc, es)
        pe, act, dve, pool = P.pe, P.act, P.dve, P.pool

        uid = [0]

        def sbt(stack, name, shape, dt):
            uid[0] += 1
            return stack.enter_context(nc.sbuf_tensor(f"{name}_{uid[0]}", shape, dt))

        vecs = sbt(es, "vecs_sb", [128, NV], F32)
        dvec = sbt(es, "dvec", [128, 24], F32)
        ident_f = sbt(es, "ident_f", [128, 128], F32)
        ident_b = sbt(es, "ident_b", [128, 128], BF16)
        ones_b = sbt(es, "ones_b", [128, 128], BF16)
        ones_f = sbt(es, "ones_f", [128, 128], F32)
        NR = 8
        ring = sbt(es, "ring", [128, NR, 1024], BF16)
        rgw = sbt(es, "rgw", [128, 2, 1024], BF16)
        TKc = max(QCH, PAST + DEC)
        cst = {"kb": 0}
        OT = sbt(es, "OT", [128, H, QCH], BF16)
        NTMP = 10
        tmpf = sbt(es, "tmpf", [128, NTMP, 512], F32)
        stat = sbt(es, "stat", [128, 16, 4], F32)
        hstate = sbt(es, "hstate", [128, 8], F32)
        xcarry = sbt(es, "xcarry", [128, 8, 3], F32)
        banks = [es.enter_context(nc.psum_tensor(f"bank{i}", [128, 512], F32)) for i in range(8)]

        def V(name):
            a, b = VC[name]
            return vecs[:, a:b]

        st = {"bank": 0, "tmp": 0, "stat": 0, "pd": 0}
        held = set()

        def bank():
            while True:
                i = st["bank"]
                st["bank"] = (i + 1) % 8
                if i not in held:
                    return i

        free_tmps = list(range(NTMP))

        def tmp():
            assert free_tmps, "out of tmp tiles"
            return free_tmps.pop(0)

        def rel(*ts):
            for t in ts:
                assert t not in free_tmps
                free_tmps.append(t)

        def need_tmps(k):
            while len(free_tmps) < k:
                yield

        def stat_slot():
            i = st["stat"]
            st["stat"] = (i + 1) % 16
            return i

        def mm(out, lhsT, rhs, start, stop, reads, writes):
            return P.op(pe, lambda: nc.tensor.matmul(out, lhsT=lhsT, rhs=rhs, start=start, stop=stop), reads, writes)

        def tr(out, in_, idn, reads, writes):
            return P.op(pe, lambda: nc.tensor.transpose(out, in_, idn), reads, writes)

        def A(eng, out, in_, func, reads, writes, **kw):
            return P.op(eng, lambda: nc.scalar.activation(out=out, in_=in_, func=func, **kw), reads, writes)

        def rstd_from_ssq(ssq_ap, ln_ap, rstd_ap, n, res):
            A(act, ln_ap, ssq_ap, AF.Ln, [res], [res], scale=1.0 / n, bias=EPS)
            A(act, rstd_ap, ln_ap, AF.Exp, [res], [res], scale=-0.5)

        chunks_of = lambda T: [(c0, min(QCH, T - c0)) for c0 in range(0, T, QCH)]
        blocks_of = lambda n: [(b0, min(512, n - b0)) for b0 in range(0, n, 512)]
        jobs = []
        for s in range(NSEQ):
            jobs.append(dict(kind="p", x=xp[s * SEQ:(s + 1) * SEQ, :], T=SEQ, past=0, y=y_p[s * SEQ:(s + 1) * SEQ, :],
                             ckv=ckv_p[s * SEQ:(s + 1) * SEQ, :], kr=kr_p[s * SEQ:(s + 1) * SEQ, :],
                             conv=conv_p[s], h=h_p[s], tcol0=0))
        jobs.append(dict(kind="s", x=xs, T=DEC, past=PAST, y=y_s, ckv=ckv_s, kr=kr_s, conv=conv_s[0], h=h_s[0],
                         tcol0=NPOS))
        sp_ = {"n": 0}

        def fetch(idx):
            i = sp_["n"]
            sp_["n"] += 1
            slot = i % NR
            P.dma(P.qw, ring[:, slot, :], WS[idx], reads=[f"WS{idx // 4}"], writes=[f"ring{slot}"])
            return ring[:, slot, :], f"ring{slot}"

        def interleave(gens, offset, max_active=MAXACT):
            active, pending, steps = [], list(gens), {}
            while active or pending:
                if pending and len(active) < max_active and (not active or steps[id(active[-1])] >= offset):
                    g = pending.pop(0)
                    active.append(g)
                    steps[id(g)] = 0
                for g in list(active):
                    try:
                        next(g)
                        steps[id(g)] += 1
                    except StopIteration:
                        active.remove(g)
                yield

        P.dma(P.qa, vecs[:], vecs_d, writes=["vecs"])
        P.dma(P.qa, ident_f[:], ident_d, writes=["ident_f"])
        P.op(dve, lambda: nc.vector.tensor_copy(out=ident_b[:], in_=ident_f[:]), ["ident_f"], ["ident_b"])
        P.op(pool, lambda: nc.gpsimd.memset(ones_b[:], 1.0), [], ["ones_b"])
        P.op(pool, lambda: nc.gpsimd.memset(ones_f[:], 1.0), [], ["ones_f"])
        a0, a1 = VC["ba"]
        P.op(dve, lambda: nc.vector.tensor_scalar(out=dvec[:, 0:16], in0=vecs[:, a0:a0 + 16], scalar1=-1.0, scalar2=None,
                                                 op0=ALU.mult), ["vecs"], ["dvec_a"])
        A(act, dvec[:, 16:24], V("lam"), AF.Exp, ["vecs"], ["dvec_c"], scale=-1.0)
        A(act, dvec[:, 16:24], dvec[:, 16:24], AF.Ln, ["dvec_c"], ["dvec_c"], scale=1.0, bias=1.0)
        P.op(dve, lambda: nc.vector.tensor_scalar(out=dvec[:, 16:24], in0=dvec[:, 16:24], scalar1=-8.0, scalar2=None,
                                                 op0=ALU.mult), ["dvec_c"], ["dvec_c"])
        CG = 4
        for g0 in range(0, NS, CG):
            g1 = min(NS, g0 + CG)
            P.dma(P.qc, WS[g0:g1], wslab[g0:g1], writes=[f"WS{g0 // CG}"])
        P.dma(P.qa, rgw[:], WS[S_W3 + W3_RG:S_W3 + W3_RG + 2].rearrange("s p f -> p s f"),
              reads=[f"WS{(S_W3 + W3_RG) // CG}", f"WS{(S_W3 + W3_RG + 1) // CG}"], writes=["rgw"])

        def tcol(jb, kidx):
            if jb["kind"] == "p":
                return kidx
            return kidx if kidx < PAST else NPOS + (kidx - PAST)

        def kr_feature_path(jb, kr_f, kr_res, nt, k0, tabs):
            tc_, ts_, tres = tabs
            b = bank()
            bk = banks[b]
            tr(bk[0:64, 0:nt], kr_f[0:nt, 0:64], ident_f[0:nt, 0:nt], [kr_res], [f"bank{b}"])
            tr(bk[0:64, 128:128 + nt], kr_f[0:nt, 64:128], ident_f[0:nt, 0:nt], [kr_res], [f"bank{b}"])
            t1, t2 = tmp(), tmp()
            P.op(dve, lambda: nc.vector.scalar_tensor_tensor(out=tmpf[0:64, t1, 0:nt], in0=bk[0:64, 0:nt],
                                                             scalar=V("gk_r")[0:64, :], in1=tc_, op0=ALU.mult, op1=ALU.mult),
                 [f"bank{b}", tres], [f"tmp{t1}"])
            P.op(dve, lambda: nc.vector.scalar_tensor_tensor(out=tmpf[0:64, t2, 0:nt], in0=bk[0:64, 128:128 + nt],
                                                             scalar=V("gk_rp")[0:64, :], in1=ts_, op0=ALU.mult, op1=ALU.mult),
                 [f"bank{b}", tres], [f"tmp{t2}"])
            P.op(pool, lambda: nc.gpsimd.tensor_tensor(out=krT[:, k0 - cst["kb"]:k0 - cst["kb"] + nt], in0=tmpf[0:64, t1, 0:nt],
                                                       in1=tmpf[0:64, t2, 0:nt], op=ALU.add),
                 [f"tmp{t1}", f"tmp{t2}"], [f"krT{k0 // 128}"])
            rel(t1, t2)
            A(act, sqkr[:, k0 - cst["kb"]:k0 - cst["kb"] + nt], bk[0:64, 0:nt], AF.Square, [f"bank{b}"], [f"sqkr{k0 // 128}"])

        def phase0(jb):
            with ExitStack() as ps_:
                cin = sbt(ps_, "cin", [128, 2, 512], F32)
                cb = sbt(ps_, "cb", [128, 2, 512], BF16)
                kin = sbt(ps_, "kin", [128, 2, 128], F32)
                tab = sbt(ps_, "tab0", [64, 2, 2, 128], F32)
                for ti in range(PAST // 128):
                    b2 = ti % 2
                    k0 = ti * 128
                    P.dma(P.qa, cin[:, b2, :], cckv[k0:k0 + 128, :], writes=[f"cin{b2}"])
                    P.dma(P.qa, kin[:, b2, :], ckr2[k0:k0 + 128, :], writes=[f"kin{b2}"])
                    P.dma(P.qa, tab[:, b2, 0, :], tabC_d[:, k0:k0 + 128], writes=[f"tab{b2}"])
                    P.dma(P.qa, tab[:, b2, 1, :], tabS_d[:, k0:k0 + 128], writes=[f"tab{b2}"])
                    P.op(dve, lambda: nc.vector.tensor_copy(out=cb[:, b2, :], in_=cin[:, b2, :]), [f"cin{b2}"], [f"cb{b2}"])
                    bi = bank()
                    bb = banks[bi][:].bitcast(BF16)
                    for kc in range(4):
                        tr(bb[:, kc * 128:(kc + 1) * 128], cb[:, b2, kc * 128:(kc + 1) * 128], ident_b[:],
                           [f"cb{b2}", "ident_b"], [f"bank{bi}"])
                    A(act, ckvT[:, :, k0 - cst["kb"]:k0 - cst["kb"] + 128], bb[:, 0:512].rearrange("p (a b) -> p a b", a=4), AF.Copy,
                      [f"bank{bi}"], [f"ckvT{ti}"])
                    kr_feature_path(jb, kin[:, b2, :], f"kin{b2}", 128, k0, (tab[:, b2, 0, :], tab[:, b2, 1, :], f"tab{b2}"))
            P.barrier()

        def phase1(jb, c0, cn):
            past = jb["past"]
            with ExitStack() as ps_:
                W1 = sbt(ps_, "W1", [128, 8, 1152], BF16)
                xin = sbt(ps_, "xin", [128, 2, 1024], F32)
                junk = sbt(ps_, "junk", [128, 2, 1024], BF16)
                xsb = sbt(ps_, "xsb", [128, 2, 1024], BF16)
                xnTt = sbt(ps_, "xnTt", [128, 2, 8, 128], BF16)
                cqs = sbt(ps_, "cqs", [128, 2, 512], BF16)
                ckvf = sbt(ps_, "ckvf", [128, 2, 512], F32)
                ckvb = sbt(ps_, "ckvb", [128, 2, 512], BF16)
                krf = sbt(ps_, "krf", [128, 2, 128], F32)
                tab = sbt(ps_, "tab1", [64, 2, 2, 128], F32)
                gkvbc = sbt(ps_, "gkvbc_sb", [128, 512], F32)
                P.dma(P.qa, gkvbc[:], gkvbc_d, writes=["gkvbc"])
                P.dma(P.qa, W1[:].rearrange("p a b -> p (a b)").rearrange("p (s f) -> p s f", s=9),
                      WS[S_W1:S_W1 + 9].rearrange("s p f -> p s f"), reads=["WS0", "WS1", "WS2"], writes=["W1"])
                nt = min(128, cn)
                def tile_gen(ti):
                    b2 = ti % 2
                    t0 = c0 + ti * nt
                    k0 = past + t0
                    tc0 = tcol(jb, k0)
                    P.dma(P.qa, xin[0:nt, b2, :], jb["x"][t0:t0 + nt, :], writes=[f"xin{b2}"])
                    P.dma(P.qa, tab[:, b2, 0, 0:nt], tabC_d[:, tc0:tc0 + nt], writes=[f"tab{b2}"])
                    P.dma(P.qa, tab[:, b2, 1, 0:nt], tabS_d[:, tc0:tc0 + nt], writes=[f"tab{b2}"])
                    ss = stat_slot()
                    sr = f"stat{ss}"
                    A(act, junk[0:nt, b2, :], xin[0:nt, b2, :], AF.Square, [f"xin{b2}"], [f"junk{b2}", sr], accum_out=stat[0:nt, ss, 0:1])
                    rstd_from_ssq(stat[0:nt, ss, 0:1], stat[0:nt, ss, 1:2], stat[0:nt, ss, 2:3], D, sr)
                    P.op(dve, lambda: nc.vector.tensor_scalar(out=xsb[0:nt, b2, :], in0=xin[0:nt, b2, :], scalar1=stat[0:nt, ss, 2:3],
                                                             scalar2=None, op0=ALU.mult), [f"xin{b2}", sr], [f"xsb{b2}"])
                    bi = bank()
                    bb = banks[bi][:].bitcast(BF16)
                    for kc in range(8):
                        tr(bb[:, kc * nt:(kc + 1) * nt], xsb[0:nt, b2, kc * 128:(kc + 1) * 128], ident_b[0:nt, 0:nt],
                           [f"xsb{b2}"], [f"bank{bi}"])
                    P.op(dve, lambda: nc.vector.tensor_tensor(out=xnTt[:, b2, :, 0:nt],
                                                             in0=bb[:, 0:8 * nt].rearrange("p (a b) -> p a b", a=8),
                                                             in1=V("g1T").unsqueeze(2).to_broadcast([128, 8, nt]), op=ALU.mult),
                         [f"bank{bi}"], [f"xnTt{b2}"])
                    yield
                    bq, bk_, br = bank(), bank(), bank()
                    for kc in range(8):
                        mm(banks[bq][0:nt, :], xnTt[:, b2, kc, 0:nt], W1[:, kc, 0:512], kc == 0, kc == 7,
                           [f"xnTt{b2}", "W1"], [f"bank{bq}"])
                    for kc in range(8):
                        mm(banks[bk_][0:nt, :], xnTt[:, b2, kc, 0:nt], W1[:, kc, 512:1024], kc == 0, kc == 7,
                           [f"xnTt{b2}", "W1"], [f"bank{bk_}"])
                    for kc in range(8):
                        mm(banks[br][0:nt, 0:128], xnTt[:, b2, kc, 0:nt], W1[:, kc, 1024:1152], kc == 0, kc == 7,
                           [f"xnTt{b2}", "W1"], [f"bank{br}"])
                    yield
                    ss = stat_slot(); sr = f"stat{ss}"
                    A(act, junk[0:nt, b2, 0:512], banks[bq][0:nt, :], AF.Square, [f"bank{bq}"], [f"junk{b2}", sr],
                      accum_out=stat[0:nt, ss, 0:1])
                    rstd_from_ssq(stat[0:nt, ss, 0:1], stat[0:nt, ss, 1:2], stat[0:nt, ss, 2:3], 512, sr)
                    P.op(dve, lambda: nc.vector.tensor_scalar(out=cqs[0:nt, b2, :], in0=banks[bq][0:nt, :], scalar1=stat[0:nt, ss, 2:3],
                                                             scalar2=None, op0=ALU.mult), [f"bank{bq}", sr], [f"cqs{b2}"])
                    bi = bank()
                    bb = banks[bi][:].bitcast(BF16)
                    for kc in range(4):
                        tr(bb[:, kc * nt:(kc + 1) * nt], cqs[0:nt, b2, kc * 128:(kc + 1) * 128], ident_b[0:nt, 0:nt],
                           [f"cqs{b2}"], [f"bank{bi}"])
                    tl = t0 - c0
                    P.op(dve, lambda: nc.vector.tensor_tensor(out=cqT[:, :, tl:tl + nt],
                                                             in0=bb[:, 0:4 * nt].rearrange("p (a b) -> p a b", a=4),
                                                             in1=V("gqT").unsqueeze(2).to_broadcast([128, 4, nt]), op=ALU.mult),
                         [f"bank{bi}"], [f"cqT{tl // 128}"])
                    yield
                    ss = stat_slot(); sr = f"stat{ss}"
                    A(act, junk[0:nt, b2, 512:1024], banks[bk_][0:nt, :], AF.Square, [f"bank{bk_}"], [f"junk{b2}", sr],
                      accum_out=stat[0:nt, ss, 0:1])
                    rstd_from_ssq(stat[0:nt, ss, 0:1], stat[0:nt, ss, 1:2], stat[0:nt, ss, 2:3], 512, sr)
                    P.op(dve, lambda: nc.vector.scalar_tensor_tensor(out=ckvf[0:nt, b2, :], in0=banks[bk_][0:nt, :],
                                                                    scalar=stat[0:nt, ss, 2:3], in1=gkvbc[0:nt, :],
                                                                    op0=ALU.mult, op1=ALU.mult),
                         [f"bank{bk_}", sr, "gkvbc"], [f"ckvf{b2}"])
                    P.dma(P.qa, jb["ckv"][t0:t0 + nt, :], ckvf[0:nt, b2, :], reads=[f"ckvf{b2}"], writes=["out_ckv"])
                    P.op(dve, lambda: nc.vector.tensor_copy(out=ckvb[0:nt, b2, :], in_=ckvf[0:nt, b2, :]), [f"ckvf{b2}"], [f"ckvb{b2}"])
                    bi = bank()
                    bb = banks[bi][:].bitcast(BF16)
                    for kc in range(4):
                        tr(bb[:, kc * nt:(kc + 1) * nt], ckvb[0:nt, b2, kc * 128:(kc + 1) * 128], ident_b[0:nt, 0:nt],
                           [f"ckvb{b2}"], [f"bank{bi}"])
                    A(act, ckvT[:, :, k0 - cst["kb"]:k0 - cst["kb"] + nt], bb[:, 0:4 * nt].rearrange("p (a b) -> p a b", a=4), AF.Copy,
                      [f"bank{bi}"], [f"ckvT{k0 // 128}"])
                    yield
                    A(act, krf[0:nt, b2, :], banks[br][0:nt, 0:128], AF.Copy, [f"bank{br}"], [f"krf{b2}"])
                    P.dma(P.qa, jb["kr"][t0:t0 + nt, :], krf[0:nt, b2, 0:64], reads=[f"krf{b2}"], writes=["out_kr"])
                    kr_feature_path(jb, krf[:, b2, :], f"krf{b2}", nt, k0, (tab[:, b2, 0, 0:nt], tab[:, b2, 1, 0:nt], f"tab{b2}"))

                for _ in interleave([tile_gen(ti) for ti in range(cn // nt)], 2):
                    pass
            P.barrier()

        def phase2(jb, c0, cn):
            past = jb["past"]
            kend = past + c0 + cn
            with ExitStack() as ps_:
                NKT = (TK + 127) // 128
                KT = sbt(ps_, "KT", [128, 2, TK], BF16)
                KR = sbt(ps_, "KR", [128, 2, TK], BF16)
                Vh = sbt(ps_, "Vh", [128, 2, NKT, 128], BF16)
                qn = sbt(ps_, "qn", [128, 4, 512], BF16)
                qr = sbt(ps_, "qr", [128, 4, 512], BF16)
                sqa = sbt(ps_, "sqa", [128, 4, 512], BF16)
                sqb = sbt(ps_, "sqb", [64, 2, 512], BF16)
                pbuf = sbt(ps_, "pbuf", [128, 4, 512], BF16)
                pdiag = sbt(ps_, "pdiag", [128, 2, 512], BF16)
                tabq = sbt(ps_, "tabq", [64, 2, QCH], F32)
                accb = sbt(ps_, "accb", [128, 2, 2, 512], F32)
                cnt = {"sq": 0, "q": 0, "p": 0, "pd": 0}
                P.op(pool, lambda: nc.gpsimd.memset(pdiag[:], 0.0), [], ["pdiag0", "pdiag1"])
                P.op(pool, lambda: nc.gpsimd.memset(KR[64:128, :, 0:kend], 0.0), [], ["KRpad"])
                P.op(pool, lambda: nc.gpsimd.memset(qr[64:128, :, :], 0.0), [], ["qrpad"])
                tq0 = tcol(jb, past + c0)
                P.dma(P.qa, tabq[:, 0, 0:cn], tabC_d[:, tq0:tq0 + cn], writes=["tabq"])
                P.dma(P.qa, tabq[:, 1, 0:cn], tabS_d[:, tq0:tq0 + cn], writes=["tabq"])

                def norm_rstd(bA, n, sq_rope_ap, sq_rope_res):
                    s4 = cnt["sq"] % 4
                    cnt["sq"] += 1
                    A(act, sqa[:, s4, 0:n], banks[bA][:, 0:n], AF.Square, [f"bank{bA}"], [f"sqa{s4}"])
                    bD = bank()
                    mm(banks[bD][:, 0:n], ones_b[:, :], sqa[:, s4, 0:n], True, False, [f"sqa{s4}", "ones_b"], [f"bank{bD}"])
                    mm(banks[bD][:, 0:n], ones_b[0:64, :], sq_rope_ap, False, True, sq_rope_res + ["ones_b"], [f"bank{bD}"])
                    tl_, tr_ = tmp(), tmp()
                    A(act, tmpf[:, tl_, 0:n], banks[bD][:, 0:n], AF.Ln, [f"bank{bD}"], [f"tmp{tl_}"], scale=1.0 / 192, bias=EPS)
                    A(act, tmpf[:, tr_, 0:n], tmpf[:, tl_, 0:n], AF.Exp, [f"tmp{tl_}"], [f"tmp{tr_}"], scale=-0.5)
                    rel(tl_)
                    return tr_

                def head_gen(h):
                    hs = h % 2
                    wq, wq_res = fetch(S_W2 + 2 * h)
                    wkv, wkv_res = fetch(S_W2 + 2 * h + 1)
                    wq3 = wq.rearrange("p (a b) -> p a b", a=4)
                    wkv3 = wkv.rearrange("p (a b) -> p a b", a=4)
                    kold = (past + c0) if (jb["kind"] == "p" and c0 > 0) else 0
                    if kold:
                        to = kold // 128
                        P.dma(P.qa, KT[:, hs, 0:kold], KTs[h, :, 0:kold], reads=[f"KTs{h}"], writes=[f"KT{hs}_{i}" for i in range(to)])
                        P.dma(P.qa, KR[0:64, hs, 0:kold], KRs[h, :, 0:kold], reads=[f"KRs{h}"], writes=[f"KR{hs}_{i}" for i in range(to)])
                        P.dma(P.qa, Vh[:, hs, 0:to, :], Vs[h, :, 0:to, :], reads=[f"Vs{h}"], writes=[f"Vh{hs}_{i}" for i in range(to)])
                    for (kb0, kn) in [(kold + b0_, n_) for (b0_, n_) in blocks_of(kend - kold)]:
                        trng = range(kb0 // 128, (kb0 + kn + 127) // 128)
                        ckv_res = [f"ckvT{i}" for i in trng]
                        bA = bank()
                        for kc in range(4):
                            mm(banks[bA][:, 0:kn], wkv3[:, kc, 0:128], ckvT[:, kc, kb0 - cst["kb"]:kb0 - cst["kb"] + kn], kc == 0, kc == 3,
                               [wkv_res] + ckv_res, [f"bank{bA}"])
                        bV = bank()
                        ntl = (kn + 127) // 128
                        ktn = min(128, kn)
                        for j in range(ntl):
                            for kc in range(4):
                                mm(banks[bV][0:ktn, j * 128:(j + 1) * 128], ckvT[:, kc, kb0 - cst["kb"] + j * 128:kb0 - cst["kb"] + j * 128 + ktn],
                                   wkv3[:, kc, 128:256], kc == 0, kc == 3, [wkv_res] + ckv_res, [f"bank{bV}"])
                        A(act, Vh[0:ktn, hs, kb0 // 128:kb0 // 128 + ntl, :],
                          banks[bV][0:ktn, 0:ntl * 128].rearrange("p (a b) -> p a b", a=ntl), AF.Copy,
                          [f"bank{bV}"], [f"Vh{hs}_{i}" for i in range(kb0 // 128, kb0 // 128 + ntl)])
                        rs = norm_rstd(bA, kn, sqkr[:, kb0 - cst["kb"]:kb0 - cst["kb"] + kn], [f"sqkr{i}" for i in trng])
                        P.op(dve, lambda: nc.vector.scalar_tensor_tensor(out=KT[:, hs, kb0:kb0 + kn], in0=banks[bA][:, 0:kn],
                                                                        scalar=V("gk_n"), in1=tmpf[:, rs, 0:kn],
                                                                        op0=ALU.mult, op1=ALU.mult),
                             [f"bank{bA}", f"tmp{rs}"], [f"KT{hs}_{i}" for i in trng])
                        P.op(dve, lambda: nc.vector.tensor_tensor(out=KR[0:64, hs, kb0:kb0 + kn], in0=krT[:, kb0 - cst["kb"]:kb0 - cst["kb"] + kn],
                                                                 in1=tmpf[0:64, rs, 0:kn], op=ALU.mult),
                             [f"tmp{rs}"] + [f"krT{i}" for i in trng], [f"KR{hs}_{i}" for i in trng])
                        rel(rs)
                        if jb["kind"] == "p" and c0 + cn < jb["T"]:
                            t0_, t1_ = kb0 // 128, (kb0 + kn) // 128
                            P.dma(P.qa, KTs[h, :, kb0:kb0 + kn], KT[:, hs, kb0:kb0 + kn], reads=[f"KT{hs}_{i}" for i in trng], writes=[f"KTs{h}"])
                            P.dma(P.qa, KRs[h, :, kb0:kb0 + kn], KR[0:64, hs, kb0:kb0 + kn], reads=[f"KR{hs}_{i}" for i in trng], writes=[f"KRs{h}"])
                            P.dma(P.qa, Vs[h, :, t0_:t1_, :], Vh[:, hs, t0_:t1_, :], reads=[f"Vh{hs}_{i}" for i in trng], writes=[f"Vs{h}"])
                        yield
                    for (qb0, qnn) in blocks_of(cn):
                        qs = hs * 2 + (qb0 // 512) % 2
                        cq_res = [f"cqT{i}" for i in range(qb0 // 128, (qb0 + qnn + 127) // 128)]
                        bA, bB, bC = bank(), bank(), bank()
                        for kc in range(4):
                            mm(banks[bA][:, 0:qnn], wq3[:, kc, 0:128], cqT[:, kc, qb0:qb0 + qnn], kc == 0, kc == 3,
                               [wq_res] + cq_res, [f"bank{bA}"])
                        for kc in range(4):
                            mm(banks[bB][0:64, 0:qnn], wq3[:, kc, 128:192], cqT[:, kc, qb0:qb0 + qnn], kc == 0, kc == 3,
                               [wq_res] + cq_res, [f"bank{bB}"])
                        for kc in range(4):
                            mm(banks[bC][0:64, 0:qnn], wq3[:, kc, 192:256], cqT[:, kc, qb0:qb0 + qnn], kc == 0, kc == 3,
                               [wq_res] + cq_res, [f"bank{bC}"])
                        A(act, sqb[:, hs, 0:qnn], banks[bB][0:64, 0:qnn], AF.Square, [f"bank{bB}"], [f"sqb{hs}"])
                        rs = norm_rstd(bA, qnn, sqb[:, hs, 0:qnn], [f"sqb{hs}"])
                        P.op(dve, lambda: nc.vector.scalar_tensor_tensor(out=qn[:, qs, 0:qnn], in0=banks[bA][:, 0:qnn],
                                                                        scalar=V("gq_n"), in1=tmpf[:, rs, 0:qnn],
                                                                        op0=ALU.mult, op1=ALU.mult),
                             [f"bank{bA}", f"tmp{rs}"], [f"qn{qs}"])
                        t1, t2 = tmp(), tmp()
                        P.op(dve, lambda: nc.vector.scalar_tensor_tensor(out=tmpf[0:64, t1, 0:qnn], in0=banks[bB][0:64, 0:qnn],
                                                                        scalar=V("gq_r")[0:64, :], in1=tabq[:, 0, qb0:qb0 + qnn],
                                                                        op0=ALU.mult, op1=ALU.mult),
                             [f"bank{bB}", "tabq"], [f"tmp{t1}"])
                        P.op(dve, lambda: nc.vector.scalar_tensor_tensor(out=tmpf[0:64, t2, 0:qnn], in0=banks[bC][0:64, 0:qnn],
                                                                        scalar=V("gq_rp")[0:64, :], in1=tabq[:, 1, qb0:qb0 + qnn],
                                                                        op0=ALU.mult, op1=ALU.mult),
                             [f"bank{bC}", "tabq"], [f"tmp{t2}"])
                        P.op(dve, lambda: nc.vector.tensor_tensor(out=tmpf[0:64, t1, 0:qnn], in0=tmpf[0:64, t1, 0:qnn],
                                                                 in1=tmpf[0:64, t2, 0:qnn], op=ALU.add),
                             [f"tmp{t1}", f"tmp{t2}"], [f"tmp{t1}"])
                        P.op(dve, lambda: nc.vector.tensor_tensor(out=qr[0:64, qs, 0:qnn], in0=tmpf[0:64, t1, 0:qnn],
                                                                 in1=tmpf[0:64, rs, 0:qnn], op=ALU.mult),
                             [f"tmp{t1}", f"tmp{rs}"], [f"qr{qs}"])
                        rel(rs, t1, t2)
                        yield
                        tiles = []
                        if jb["kind"] == "p":
                            qg = (c0 + qb0) // 512
                            for kt in range(4 * qg):
                                tiles.append((kt, 128, None))
                            for jj in range((qnn + 127) // 128):
                                tiles.append((4 * qg + jj, 128, jj))
                        else:
                            for kt in range((kend + 127) // 128):
                                tiles.append((kt, min(128, kend - kt * 128), None))
                        bO = bank()
                        held.add(bO)
                        sb_ = {}
                        accD, accP = accb[:, hs, 0, :], accb[:, hs, 1, :]
                        rD, rP = f"accD{hs}", f"accP{hs}"
                        P.op(dve, lambda: nc.vector.memset(accD, 0.0), [], [rD])
                        P.op(pool, lambda: nc.gpsimd.memset(accP, 0.0), [], [rP])

                        def emit_S(i):
                            kt, ktn, jj = tiles[i]
                            cc0 = 0 if jj is None else 128 * jj
                            W = qnn - cc0
                            b = bank()
                            sb_[i] = b
                            mm(banks[b][0:ktn, 0:W], KT[:, hs, kt * 128:kt * 128 + ktn], qn[:, qs, cc0:qnn], True, False,
                               [f"KT{hs}_{kt}", f"qn{qs}"], [f"bank{b}"])
                            mm(banks[b][0:ktn, 0:W], KR[:, hs, kt * 128:kt * 128 + ktn], qr[:, qs, cc0:qnn], False, True,
                               [f"KR{hs}_{kt}", f"qr{qs}", "KRpad", "qrpad"], [f"bank{b}"])

                        def emit_PV(i):
                            kt, ktn, jj = tiles[i]
                            cc0 = 0 if jj is None else 128 * jj
                            W = qnn - cc0
                            b = sb_.pop(i)
                            if jj is None:
                                s4 = cnt["p"] % 4
                                cnt["p"] += 1
                                pap, pres = pbuf[:, s4, :], f"pbuf{s4}"
                                A(act, pap[0:ktn, 0:W], banks[b][0:ktn, 0:W], AF.Exp, [f"bank{b}"], [pres], scale=SCALE)
                            else:
                                s2 = cnt["pd"] % 2
                                cnt["pd"] += 1
                                pap, pres = pdiag[:, s2, :], f"pdiag{s2}"
                                A(act, pap[0:64, 0:W], banks[b][0:64, 0:W], AF.Exp, [f"bank{b}"], [pres], scale=SCALE)
                                if W > 64:
                                    A(act, pap[64:128, 64:W], banks[b][64:128, 64:W], AF.Exp, [f"bank{b}"], [pres], scale=SCALE)
                            first, last = (i == 0), (i == len(tiles) - 1)
                            mm(banks[bO][:, cc0:qnn], Vh[0:ktn, hs, kt, :], pap[0:ktn, 0:W], first, last,
                               [f"Vh{hs}_{kt}", pres], [f"bank{bO}"])
                            if i % 3 != 2:
                                P.op(dve, lambda: nc.vector.tensor_tensor(out=accD[0:ktn, cc0:qnn], in0=accD[0:ktn, cc0:qnn],
                                                                         in1=pap[0:ktn, 0:W], op=ALU.add), [pres, rD], [rD])
                            else:
                                P.op(pool, lambda: nc.gpsimd.tensor_tensor(out=accP[0:ktn, cc0:qnn], in0=accP[0:ktn, cc0:qnn],
                                                                           in1=pap[0:ktn, 0:W], op=ALU.add), [pres, rP], [rP])

                        nT = len(tiles)
                        for i in range(min(2, nT)):
                            emit_S(i)
                        for i in range(nT):
                            emit_PV(i)
                            if i + 2 < nT:
                                emit_S(i + 2)
                            if i % 2 == 1:
                                yield
                        bD = bank()
                        mm(banks[bD][:, 0:qnn], ones_f[:, :], accD[:, 0:qnn], True, False, [rD], [f"bank{bD}"])
                        mm(banks[bD][:, 0:qnn], ones_f[:, :], accP[:, 0:qnn], False, True, [rP], [f"bank{bD}"])
                        rd = tmp()
                        A(act, tmpf[:, rd, 0:qnn], banks[bD][:, 0:qnn], AF.Ln, [f"bank{bD}"], [f"tmp{rd}"])
                        A(act, tmpf[:, rd, 0:qnn], tmpf[:, rd, 0:qnn], AF.Exp, [f"tmp{rd}"], [f"tmp{rd}"], scale=-1.0)
                        P.op(dve, lambda: nc.vector.tensor_tensor(out=OT[:, h, qb0:qb0 + qnn], in0=banks[bO][:, 0:qnn],
                                                                 in1=tmpf[:, rd, 0:qnn], op=ALU.mult),
                             [f"bank{bO}", f"tmp{rd}"], [f"OT{h}_{qb0 // 512}"])
                        rel(rd)
                        held.discard(bO)
                        yield

                nkb = len(blocks_of(kend - ((past + c0) if (jb["kind"] == "p" and c0 > 0) else 0)))
                for _ in interleave([head_gen(h) for h in range(H)], nkb + 1, 2):
                    pass
            P.barrier()

        def phase3(jb, c0, cn, first_chunk, last_chunk):
            with ExitStack() as ps_:
                xy = sbt(ps_, "xy", [128, 2, 4, 1024], F32)
                XN = sbt(ps_, "XN", [128, 2, 8, 512], BF16)
                RZ = sbt(ps_, "RZ", [128, 44, 512], BF16)
                xsb = sbt(ps_, "xsb3", [128, 2, 1024], BF16)
                xr = sbt(ps_, "xr", [128, 2, 515], F32)
                xcb = sbt(ps_, "xcb", [128, 2, 512], BF16)
                blks = blocks_of(cn)
                cw0 = VC["convw"][0]
                cnt3 = {"xs": 0}

                def rz(s, k):
                    return k + 22 * s

                def load_x(bi_, b0, n):
                    nt = min(128, n)
                    for t in range(n // nt):
                        P.dma(P.qa, xy[0:nt, bi_ % 2, t, :], jb["x"][c0 + b0 + t * nt:c0 + b0 + (t + 1) * nt, :],
                              writes=[f"xy{bi_ % 2}_{t}"])

                def norm_T(src_ap, src_res, nt, t, gname, s):
                    ss = stat_slot(); sr = f"stat{ss}"
                    xs_ = cnt3["xs"] % 2
                    cnt3["xs"] += 1
                    A(act, xsb[0:nt, xs_, :], src_ap, AF.Square, [src_res], [f"xsb{xs_}", sr], accum_out=stat[0:nt, ss, 0:1])
                    rstd_from_ssq(stat[0:nt, ss, 0:1], stat[0:nt, ss, 1:2], stat[0:nt, ss, 2:3], D, sr)
                    P.op(dve, lambda: nc.vector.tensor_scalar(out=xsb[0:nt, xs_, :], in0=src_ap, scalar1=stat[0:nt, ss, 2:3],
                                                             scalar2=None, op0=ALU.mult), [src_res, sr], [f"xsb{xs_}"])
                    bi = bank()
                    bb = banks[bi][:].bitcast(BF16)
                    for kc in range(8):
                        tr(bb[:, kc * nt:(kc + 1) * nt], xsb[0:nt, xs_, kc * 128:(kc + 1) * 128], ident_b[0:nt, 0:nt],
                           [f"xsb{xs_}"], [f"bank{bi}"])
                    P.op(dve, lambda: nc.vector.tensor_tensor(out=XN[:, s, :, t * nt:(t + 1) * nt],
                                                             in0=bb[:, 0:8 * nt].rearrange("p (a b) -> p a b", a=8),
                                                             in1=V(gname).unsqueeze(2).to_broadcast([128, 8, nt]), op=ALU.mult),
                         [f"bank{bi}"], [f"XN{s}_{kc}" for kc in range(8)])

                def sigmoid_chain(pairs, n):
                    for tt in pairs:
                        A(act, tmpf[:, tt, 0:n], tmpf[:, tt, 0:n], AF.Ln, [f"tmp{tt}"], [f"tmp{tt}"], scale=1.0, bias=1.0)
                    for tt in pairs:
                        A(act, tmpf[:, tt, 0:n], tmpf[:, tt, 0:n], AF.Exp, [f"tmp{tt}"], [f"tmp{tt}"], scale=-1.0)

                def block_gen(bi_, b0, n):
                    s = bi_ % 2
                    nt = min(128, n)
                    ntile = n // nt
                    first_blk = first_chunk and bi_ == 0
                    last_blk = last_chunk and bi_ == len(blks) - 1
                    XNr = [f"XN{s}_{kc}" for kc in range(8)]
                    if bi_ >= 2:
                        load_x(bi_, b0, n)
                    RZr = lambda k: f"RZ{rz(s, k)}"
                    RZa = lambda k: RZ[:, rz(s, k), :]
                    for t in range(ntile):
                        norm_T(xy[0:nt, s, t, :], f"xy{s}_{t}", nt, t, "g1T", s)
                        yield
                    rga3 = rgw[:, 0, :].rearrange("p (a b) -> p a b", a=8)
                    rgx3 = rgw[:, 1, :].rearrange("p (a b) -> p a b", a=8)
                    if first_blk:
                        if jb["kind"] == "p":
                            P.op(pool, lambda: nc.gpsimd.memset(xcarry[:], 0.0), [], ["xcarry"])
                            P.op(pool, lambda: nc.gpsimd.memset(hstate[:], 0.0), [], ["hstate"])
                        else:
                            P.op(pool, lambda: nc.gpsimd.tensor_copy(out=xcarry[:].rearrange("p a b -> p (a b)"), in_=V("conv0")),
                                 ["vecs"], ["xcarry"])
                            P.op(pool, lambda: nc.gpsimd.tensor_copy(out=hstate[:], in_=V("h0")), ["vecs"], ["hstate"])

                    def rnn_gen(chs):
                        for nch in chs:
                            yield from need_tmps(4)
                            w3, wres = fetch(S_W3 + W3_XR + nch)
                            w3 = w3.rearrange("p (a b) -> p a b", a=8)
                            xs_ = nch % 2
                            bx = bank()
                            for kc in range(8):
                                mm(banks[bx][:, 0:n], w3[:, kc, :], XN[:, s, kc, 0:n], kc == 0, kc == 7, [wres] + XNr, [f"bank{bx}"])
                            P.op(pool, lambda: nc.gpsimd.tensor_copy(out=xr[:, xs_, 0:3], in_=xcarry[:, nch, :]), [f"xcarry{nch}"], [f"xr{xs_}"])
                            A(act, xr[:, xs_, 3:3 + n], banks[bx][:, 0:n], AF.Copy, [f"bank{bx}"], [f"xr{xs_}"])
                            P.op(pool, lambda: nc.gpsimd.tensor_copy(out=xcarry[:, nch, :], in_=xr[:, xs_, n:n + 3]), [f"xr{xs_}"], [f"xcarry{nch}"])
                            xc = tmp()
                            cb = cw0 + nch * 4
                            P.op(dve, lambda: nc.vector.tensor_scalar(out=tmpf[:, xc, 0:n], in0=xr[:, xs_, 0:n],
                                                                     scalar1=vecs[:, cb:cb + 1], scalar2=V("convb")[:, nch:nch + 1],
                                                                     op0=ALU.mult, op1=ALU.add), [f"xr{xs_}", "vecs"], [f"tmp{xc}"])
                            for j in range(1, 4):
                                P.op(dve, lambda: nc.vector.scalar_tensor_tensor(out=tmpf[:, xc, 0:n], in0=xr[:, xs_, j:j + n],
                                                                                scalar=vecs[:, cb + j:cb + j + 1], in1=tmpf[:, xc, 0:n],
                                                                                op0=ALU.mult, op1=ALU.add),
                                     [f"xr{xs_}", f"tmp{xc}"], [f"tmp{xc}"])
                            P.op(dve, lambda: nc.vector.tensor_copy(out=xcb[:, xs_, 0:n], in_=tmpf[:, xc, 0:n]), [f"tmp{xc}"], [f"xcb{xs_}"])
                            yield
                            yield from need_tmps(3)
                            br_, bi2 = bank(), bank()
                            mm(banks[br_][:, 0:n], rga3[:, nch, :], xcb[:, xs_, 0:n], True, True, ["rgw", f"xcb{xs_}"], [f"bank{br_}"])
                            mm(banks[bi2][:, 0:n], rgx3[:, nch, :], xcb[:, xs_, 0:n], True, True, ["rgw", f"xcb{xs_}"], [f"bank{bi2}"])
                            tr_, ti_ = tmp(), tmp()
                            A(act, tmpf[:, tr_, 0:n], banks[br_][:, 0:n], AF.Exp, [f"bank{br_}", "dvec_a"], [f"tmp{tr_}"],
                              scale=-1.0, bias=dvec[:, nch:nch + 1])
                            A(act, tmpf[:, ti_, 0:n], banks[bi2][:, 0:n], AF.Exp, [f"bank{bi2}", "dvec_a"], [f"tmp{ti_}"],
                              scale=-1.0, bias=dvec[:, 8 + nch:9 + nch])
                            sigmoid_chain((tr_, ti_), n)
                            yield
                            ta, tm = tmp(), tr_
                            A(act, tmpf[:, ta, 0:n], tmpf[:, tr_, 0:n], AF.Exp, [f"tmp{tr_}", "dvec_c"], [f"tmp{ta}"],
                              scale=dvec[:, 16 + nch:17 + nch])
                            P.op(dve, lambda: nc.vector.tensor_tensor(out=tmpf[:, tm, 0:n], in0=tmpf[:, ta, 0:n], in1=tmpf[:, ta, 0:n],
                                                                     op=ALU.mult), [f"tmp{ta}"], [f"tmp{tm}"])
                            A(act, tmpf[:, tm, 0:n], tmpf[:, tm, 0:n], AF.Ln, [f"tmp{tm}"], [f"tmp{tm}"], scale=-1.0, bias=1.0)
                            A(act, tmpf[:, tm, 0:n], tmpf[:, tm, 0:n], AF.Exp, [f"tmp{tm}"], [f"tmp{tm}"], scale=0.5)
                            P.op(dve, lambda: nc.vector.tensor_tensor(out=tmpf[:, ti_, 0:n], in0=tmpf[:, ti_, 0:n], in1=tmpf[:, xc, 0:n],
                                                                     op=ALU.mult), [f"tmp{ti_}", f"tmp{xc}"], [f"tmp{ti_}"])
                            P.op(dve, lambda: nc.vector.tensor_tensor(out=tmpf[:, ti_, 0:n], in0=tmpf[:, ti_, 0:n], in1=tmpf[:, tm, 0:n],
                                                                     op=ALU.mult), [f"tmp{ti_}", f"tmp{tm}"], [f"tmp{ti_}"])
                            P.op(dve, lambda: nc.vector.tensor_tensor_scan(out=tmpf[:, xc, 0:n], data0=tmpf[:, ta, 0:n],
                                                                          data1=tmpf[:, ti_, 0:n], initial=hstate[:, nch:nch + 1],
                                                                          op0=ALU.mult, op1=ALU.add),
                                 [f"tmp{ta}", f"tmp{ti_}", f"hstate{nch}", f"tmp{xc}"], [f"tmp{xc}"])
                            P.op(pool, lambda: nc.gpsimd.tensor_copy(out=hstate[:, nch:nch + 1], in_=tmpf[:, xc, n - 1:n]),
                                 [f"tmp{xc}"], [f"hstate{nch}"])
                            A(act, RZa(nch)[:, 0:n], tmpf[:, xc, 0:n], AF.Copy, [f"tmp{xc}"], [RZr(nch)])
                            rel(xc, tr_, ti_, ta)
                            yield

                    if first_blk:
                        for nch in range(8):
                            P.lastw[f"xcarry{nch}"] = P.lastw["xcarry"]
                            P.lastw[f"hstate{nch}"] = P.lastw["hstate"]
                    yield from interleave([rnn_gen(range(0, 8, 2)), rnn_gen(range(1, 8, 2))], 1)
                    if last_blk:
                        P.dma(P.qa, jb["conv"], xcarry[:].rearrange("p a b -> p (a b)"), reads=[f"xcarry{k}" for k in range(8)],
                              writes=["out_conv"])
                        P.dma(P.qa, jb["h"], hstate[:], reads=[f"hstate{k}" for k in range(8)], writes=["out_h"])
                    qbi = b0 // 512
                    for j in range(8):
                        base = S_W3 + W3_MG + 4 * j
                        tparts = []
                        yield from need_tmps(2)
                        for part, (gi, oi) in enumerate(((0, 2), (1, 3))):
                            gsl, gres = fetch(base + gi)
                            osl, ores = fetch(base + oi)
                            g3 = gsl.rearrange("p (a b) -> p a b", a=8)
                            o3 = osl.rearrange("p (a b) -> p a b", a=8)
                            bg, bo = bank(), bank()
                            for kc in range(8):
                                mm(banks[bg][:, 0:n], g3[:, kc, :], XN[:, s, kc, 0:n], kc == 0, kc == 7, [gres] + XNr, [f"bank{bg}"])
                            for kc in range(8):
                                if part == 0:
                                    rhs, rres = OT[:, kc, b0:b0 + n], [f"OT{kc}_{qbi}"]
                                else:
                                    rhs, rres = RZa(kc)[:, 0:n], [RZr(kc)]
                                mm(banks[bo][:, 0:n], o3[:, kc, :], rhs, kc == 0, kc == 7, [ores] + rres, [f"bank{bo}"])
                            te = tmp()
                            A(act, tmpf[:, te, 0:n], banks[bg][:, 0:n], AF.Exp, [f"bank{bg}"], [f"tmp{te}"], scale=-1.0)
                            sigmoid_chain((te,), n)
                            P.op(dve, lambda: nc.vector.tensor_tensor(out=tmpf[:, te, 0:n], in0=banks[bo][:, 0:n], in1=tmpf[:, te, 0:n],
                                                                     op=ALU.mult), [f"bank{bo}", f"tmp{te}"], [f"tmp{te}"])
                            tparts.append(te)
                            yield
                        ea, eg = tparts
                        P.op(pool, lambda: nc.gpsimd.tensor_tensor(out=RZa(8 + j)[:, 0:n], in0=tmpf[:, ea, 0:n], in1=tmpf[:, eg, 0:n],
                                                                   op=ALU.add), [f"tmp{ea}", f"tmp{eg}"], [RZr(8 + j)])
                        rel(ea, eg)

                    def acc_stage(nslab, slab_base, kc_of, src_k, last_kc):
                        for half in range(2):
                            while len(held) + ntile > 6:
                                yield
                            bs = [bank() for _ in range(ntile)]
                            for b_ in bs:
                                held.add(b_)
                            for si in range(nslab):
                                wsl, wres = fetch(slab_base + half * nslab + si)
                                w3 = wsl.rearrange("p (a b) -> p a b", a=2)
                                for kk in range(2):
                                    kc = 2 * si + kk
                                    for t in range(ntile):
                                        mm(banks[bs[t]][0:nt, :], RZa(src_k(kc))[:, t * nt:(t + 1) * nt], w3[:, kk, :], kc == 0, kc == last_kc,
                                           [wres, RZr(src_k(kc))], [f"bank{bs[t]}"])
                                yield
                            for t in range(ntile):
                                P.op(dve, lambda: nc.vector.tensor_tensor(out=xy[0:nt, s, t, half * 512:(half + 1) * 512],
                                                                         in0=banks[bs[t]][0:nt, :],
                                                                         in1=xy[0:nt, s, t, half * 512:(half + 1) * 512], op=ALU.add),
                                     [f"bank{bs[t]}", f"xy{s}_{t}"], [f"xy{s}_{t}"])
                                held.discard(bs[t])
                            yield

                    yield from acc_stage(4, S_W3 + W3_OUT, None, lambda kc: 8 + kc, 7)
                    for t in range(ntile):
                        norm_T(xy[0:nt, s, t, :], f"xy{s}_{t}", nt, t, "g2T", s)
                        yield
                    for j in range(NFF):
                        yield from need_tmps(1)
                        gsl, gres = fetch(S_W3 + W3_FF + 2 * j)
                        usl, ures = fetch(S_W3 + W3_FF + 2 * j + 1)
                        g3 = gsl.rearrange("p (a b) -> p a b", a=8)
                        u3 = usl.rearrange("p (a b) -> p a b", a=8)
                        bg, bu = bank(), bank()
                        for kc in range(8):
                            mm(banks[bg][:, 0:n], g3[:, kc, :], XN[:, s, kc, 0:n], kc == 0, kc == 7, [gres] + XNr, [f"bank{bg}"])
                        for kc in range(8):
                            mm(banks[bu][:, 0:n], u3[:, kc, :], XN[:, s, kc, 0:n], kc == 0, kc == 7, [ures] + XNr, [f"bank{bu}"])
                        te = tmp()
                        A(act, tmpf[:, te, 0:n], banks[bg][:, 0:n], AF.Exp, [f"bank{bg}"], [f"tmp{te}"], scale=-1.0)
                        sigmoid_chain((te,), n)
                        P.op(dve, lambda: nc.vector.tensor_tensor(out=tmpf[:, te, 0:n], in0=banks[bg][:, 0:n], in1=tmpf[:, te, 0:n],
                                                                 op=ALU.mult), [f"bank{bg}", f"tmp{te}"], [f"tmp{te}"])
                        P.op(dve, lambda: nc.vector.tensor_tensor(out=RZa(j)[:, 0:n], in0=banks[bu][:, 0:n], in1=tmpf[:, te, 0:n],
                                                                 op=ALU.mult), [f"bank{bu}", f"tmp{te}"], [RZr(j)])
                        rel(te)
                        yield
                    yield from acc_stage(11, S_W3 + W3_DN, None, lambda kc: kc, NFF - 1)
                    for t in range(ntile):
                        P.dma(P.qa, jb["y"][c0 + b0 + t * nt:c0 + b0 + (t + 1) * nt, :], xy[0:nt, s, t, :],
                              reads=[f"xy{s}_{t}"], writes=["out_y"])

                for bi_, (b0, n) in enumerate(blks[:2]):
                    load_x(bi_, b0, n)
                for _ in interleave([block_gen(bi_, b0, n) for bi_, (b0, n) in enumerate(blks)], P3_OFF):
                    pass
            P.barrier()

        for jb in jobs:
            chs = chunks_of(jb["T"])
            for ci, (c0, cn) in enumerate(chs):
                with ExitStack() as cs_:
                    cqT = sbt(cs_, "cqT", [128, 4, QCH], BF16)
                    ckvT = sbt(cs_, "ckvT", [128, 4, TKc], BF16)
                    krT = sbt(cs_, "krT", [64, TKc], BF16)
                    sqkr = sbt(cs_, "sqkr", [64, TKc], BF16)
                    cst["kb"] = c0 if jb["kind"] == "p" else 0
                    if jb["kind"] == "s":
                        phase0(jb)
                    phase1(jb, c0, cn)
                    phase2(jb, c0, cn)
                phase3(jb, c0, cn, ci == 0, ci == len(chs) - 1)
        P.barrier(engines=[P.sp, pool], queues=[P.qa, P.qw])
    return nc


def _slabs(inp):
    w_in = inp["w_in"][0]; w_uq = inp["w_uq"][0]; w_ukv = inp["w_ukv"][0]
    S = np.empty((NS, 128, 1024), np.float32)
    perm = np.concatenate([np.arange(32, 64), np.arange(0, 32)])

    def kc_cols(w, cols):
        K = w.shape[0]
        return w[:, cols].reshape(K // 128, 128, len(cols)).transpose(1, 0, 2)

    kr_cols = np.arange(OFF_KR, OFF_KR + 64)
    cols1 = np.concatenate([np.arange(0, 1024), kr_cols, kr_cols[perm]])
    S[0:9] = kc_cols(w_in, cols1).reshape(128, 9, 1024).transpose(1, 0, 2)
    for h in range(H):
        qc = np.arange(h * 192, h * 192 + 192)
        cols = np.concatenate([qc[:128], qc[128:], qc[128:][perm]])
        S[S_W2 + 2 * h] = kc_cols(w_uq, cols).reshape(128, 1024)
        kvc = np.arange(h * 256, h * 256 + 256)
        S[S_W2 + 2 * h + 1] = kc_cols(w_ukv, kvc).reshape(128, 1024)
    b = S_W3
    for n in range(8):
        S[b + W3_XR + n] = kc_cols(w_in, np.arange(OFF_X + n * 128, OFF_X + (n + 1) * 128)).reshape(128, 1024)
    S[b + W3_RG] = inp["w_rg_a"][0].transpose(1, 0, 2).reshape(128, 1024)
    S[b + W3_RG + 1] = inp["w_rg_x"][0].transpose(1, 0, 2).reshape(128, 1024)
    wpa = inp["w_proj_attn"][0]; wpr = inp["w_proj_rnn"][0]
    for j in range(8):
        cj = np.arange(j * 128, (j + 1) * 128)
        S[b + W3_MG + 4 * j + 0] = kc_cols(w_in, OFF_GA + cj).reshape(128, 1024)
        S[b + W3_MG + 4 * j + 1] = kc_cols(w_in, OFF_GB + cj).reshape(128, 1024)
        S[b + W3_MG + 4 * j + 2] = kc_cols(wpa, cj).reshape(128, 1024)
        S[b + W3_MG + 4 * j + 3] = kc_cols(wpr, cj).reshape(128, 1024)
    wo = inp["w_out"][0]
    for half in range(2):
        for s4 in range(4):
            S[b + W3_OUT + half * 4 + s4] = wo[s4 * 256:(s4 + 1) * 256, half * 512:(half + 1) * 512].reshape(2, 128, 512).transpose(1, 0, 2).reshape(128, 1024)
    wg = inp["w_ffn_gate"][0]; wu = inp["w_ffn_up"][0]; wd = inp["w_ffn_down"][0]
    for j in range(NFF):
        cj = np.arange(j * 128, (j + 1) * 128)
        S[b + W3_FF + 2 * j] = kc_cols(wg, cj).reshape(128, 1024)
        S[b + W3_FF + 2 * j + 1] = kc_cols(wu, cj).reshape(128, 1024)
    for half in range(2):
        for s11 in range(11):
            S[b + W3_DN + half * 11 + s11] = wd[s11 * 256:(s11 + 1) * 256, half * 512:(half + 1) * 512].reshape(2, 128, 512).transpose(1, 0, 2).reshape(128, 1024)
    return S


def _vecs_common(inp):
    v = np.zeros((128, NV), np.float32)
    perm = np.concatenate([np.arange(32, 64), np.arange(0, 32)])

    def put(name, arr):
        a, b = VC[name]
        arr = np.asarray(arr, np.float32)
        v[:arr.shape[0], a:b] = arr.reshape(arr.shape[0], b - a)

    put("g1T", inp["norm1_g"][0].reshape(8, 128).T)
    put("g2T", inp["norm2_g"][0].reshape(8, 128).T)
    put("gqT", inp["q_norm_g"][0].reshape(4, 128).T)
    gq = inp["qk_q_g"][0]; gk = inp["qk_k_g"][0]
    put("gq_n", gq[:128, None]); put("gq_r", gq[128:, None]); put("gq_rp", gq[128:][perm][:, None])
    put("gk_n", gk[:128, None]); put("gk_r", gk[128:, None]); put("gk_rp", gk[128:][perm][:, None])
    put("convw", inp["conv_w"][0].reshape(4, 8, 128).transpose(2, 1, 0).reshape(128, 32))
    put("convb", inp["conv_b"][0].reshape(8, 128).T)
    put("ba", inp["b_rg_a"][0].reshape(8, 128).T)
    put("bx", inp["b_rg_x"][0].reshape(8, 128).T)
    put("lam", inp["lru_lambda"][0].reshape(8, 128).T)
    return v


def _rope_tables(seq):
    half = 32
    inv_freq = np.exp(-math.log(10000.0) * np.arange(half, dtype=np.float32) / half).astype(np.float32)
    pos = np.concatenate([np.arange(max(seq, PAST), dtype=np.float32), PAST + np.arange(DEC, dtype=np.float32)])
    ang = (pos[:, None] * inv_freq[None, :]).astype(np.float32)
    c = np.cos(ang).astype(np.float32).T
    s = np.sin(ang).astype(np.float32).T
    return np.ascontiguousarray(np.concatenate([c, c], 0)), np.ascontiguousarray(np.concatenate([-s, s], 0))


_CACHE = {}


def _in_maps(inp, SEQ, NSEQ, ncores):
    S = _slabs(inp)
    vc = _vecs_common(inp)
    tabC, tabS = _rope_tables(SEQ)
    ident = np.eye(128, dtype=np.float32)
    gkvbc = np.ascontiguousarray(np.broadcast_to(inp["kv_norm_g"][0][None, :], (128, 512))).astype(np.float32)
    perm = np.concatenate([np.arange(32, 64), np.arange(0, 32)])
    in_maps = []
    for c in range(ncores):
        v = vc.copy()
        a, b = VC["conv0"]
        v[:, a:b] = inp["state_conv"][0, c].reshape(3, 8, 128).transpose(2, 1, 0).reshape(128, 24)
        a, b = VC["h0"]
        v[:, a:b] = inp["state_h"][0, c].reshape(8, 128).T
        kr = inp["cache_krope"][0, c]
        in_maps.append(dict(
            xp=np.ascontiguousarray(inp["x_prompt"][c * NSEQ:(c + 1) * NSEQ].reshape(NSEQ * SEQ, D)),
            xs=np.ascontiguousarray(inp["x_sample"][c]),
            cckv=np.ascontiguousarray(inp["cache_ckv"][0, c]),
            ckr2=np.ascontiguousarray(np.concatenate([kr, kr[:, perm]], axis=1)),
            vecs=v, gkvbc=gkvbc, ident=ident, tabC=tabC, tabS=tabS, wslab=S))
    return in_maps


def _run(inp, SEQ, NSEQ, QCH, ncores):
    key = (SEQ, NSEQ, QCH)
    if key not in _CACHE:
        _CACHE[key] = build_program(SEQ, NSEQ, QCH)
    nc = _CACHE[key]
    in_maps = _in_maps(inp, SEQ, NSEQ, ncores)
    res = run_bass_kernel_spmd(nc, in_maps, core_ids=list(range(ncores)))
    return _assemble(res.results, SEQ, NSEQ)


def _assemble(R, SEQ, NSEQ):
    y_p = np.concatenate([r["y_p"].reshape(NSEQ, SEQ, D) for r in R], 0)
    y_s = np.stack([r["y_s"] for r in R], 0)
    ckv_p = np.concatenate([r["ckv_p"].reshape(NSEQ, SEQ, 512) for r in R], 0)[None]
    kr_p = np.concatenate([r["kr_p"].reshape(NSEQ, SEQ, 64) for r in R], 0)[None]

    def conv_fix(a):
        n = a.shape[0]
        return a.reshape(n, 128, 8, 3).transpose(0, 3, 2, 1).reshape(n, 3, 1024)

    def h_fix(a):
        n = a.shape[0]
        return a.transpose(0, 2, 1).reshape(n, 1024)

    conv_pp = np.concatenate([conv_fix(r["conv_p"]) for r in R], 0)[None]
    h_pp = np.concatenate([h_fix(r["h_p"]) for r in R], 0)[None]
    ckv_s = np.stack([r["ckv_s"] for r in R], 0)[None]
    kr_s = np.stack([r["kr_s"] for r in R], 0)[None]
    conv_s = np.concatenate([conv_fix(r["conv_s"]) for r in R], 0)[None]
    h_s = np.concatenate([h_fix(r["h_s"]) for r in R], 0)[None]
    outs = (y_p, y_s, ckv_p, kr_p, conv_pp, h_pp, ckv_s, kr_s, conv_s, h_s)
    return tuple(np.ascontiguousarray(o, dtype=np.float32) for o in outs)


def kernel(**inputs):
    inp = {k: np.asarray(v) for k, v in inputs.items()}
    return _run(inp, 4096, 2, 2048, NCORES)
```
